# Optimizing a Trainium2 kernel written in Bass

```python
import math
import jax
import jax.numpy as jnp
from jax import lax
import numpy as np

D_MODEL = 1024
BATCH = 8
SEQ = 4096
DEPTH = 1

HEAD_DIM = 64
A_HEADS = 8
DIL_PATTERNS = ((128, 1), (512, 4), (2048, 16))
DIL_BLOCK = 128
B_HEADS = 8
B_KV_HEADS = 2
B_GROUP = B_HEADS // B_KV_HEADS
CMP_LEN = 32
CMP_STRIDE = 16
CMP_HIDDEN = 128
SEL_BLOCK = 64
SEL_TOPN = 16
SEL_Q_CHUNK = 64
CMP_OVERLAP = (1.0, 2.0, 2.0, 2.0, 1.0)
WIN_LEN = 512
WIN_BLOCK = 128
N_BRANCH = 3
REL_BUCKETS = 32
REL_MAX_DIST = 2048
N_HEADS_TOTAL = A_HEADS + B_HEADS
FFN_HIDDEN = ((-(-8 * D_MODEL // 3)) + 255) // 256 * 256
RMS_EPS = 1e-6
A_WIDTH = A_HEADS * HEAD_DIM
B_WIDTH = B_HEADS * HEAD_DIM
KV_WIDTH = B_KV_HEADS * HEAD_DIM
GATE_WIDTH = B_HEADS * N_BRANCH
IN_SPLITS = (A_WIDTH, A_WIDTH, A_WIDTH, B_WIDTH, KV_WIDTH, KV_WIDTH, KV_WIDTH, KV_WIDTH, KV_WIDTH, KV_WIDTH, GATE_WIDTH)
IN_WIDTH = sum(IN_SPLITS)
SCALE = HEAD_DIM ** -0.5

kernel_name = "hybrid_dilated_nsa_layer"


def rms_norm(x, g):
    xf = x.astype(jnp.float32)
    y = xf * lax.rsqrt(jnp.mean(xf * xf, axis=-1, keepdims=True) + RMS_EPS)
    return (y * g.astype(jnp.float32)).astype(x.dtype)


def t5_bucket(dist):
    max_exact = REL_BUCKETS // 2
    df = jnp.maximum(dist, 1).astype(jnp.float32)
    large = max_exact + (jnp.log(df / max_exact) / math.log(REL_MAX_DIST / max_exact)
                         * (REL_BUCKETS - max_exact)).astype(jnp.int32)
    large = jnp.minimum(large, REL_BUCKETS - 1)
    return jnp.where(dist < max_exact, dist, large)


def masked_softmax(logits, mask):
    logits = jnp.where(mask, logits, -jnp.inf)
    m = jnp.max(logits, axis=-1, keepdims=True)
    m = jnp.where(jnp.isfinite(m), m, 0.0)
    p = jnp.exp(logits - m)
    den = jnp.sum(p, axis=-1, keepdims=True)
    return p / jnp.maximum(den, 1e-30)


def split_columns(t):
    out, start = [], 0
    for w in IN_SPLITS:
        out.append(t[..., start:start + w])
        start += w
    return out


def to_heads(t, n):
    b, s, _ = t.shape
    return t.reshape(b, s, n, HEAD_DIM).transpose(0, 2, 1, 3)


def dilated_attention(q, k, v, table, window, dil):
    b, h, s, dh = q.shape
    steps = window // dil
    unit = dil * DIL_BLOCK
    sp = -(-s // unit) * unit
    length = sp // dil
    nb = length // DIL_BLOCK

    def to_blocks(t):
        t = jnp.pad(t, ((0, 0), (0, 0), (0, sp - s), (0, 0)))
        t = t.reshape(b, h, length, dil, dh).transpose(0, 1, 3, 2, 4)
        return t.reshape(b, h, dil, nb, DIL_BLOCK, dh)

    def with_prev(t):
        prev = jnp.pad(t[:, :, :, :-1], ((0, 0), (0, 0), (0, 0), (1, 0), (0, 0), (0, 0)))
        return jnp.concatenate([prev, t], axis=4)

    qb = to_blocks(q)
    kk = with_prev(to_blocks(k))
    vv = with_prev(to_blocks(v))
    i = jnp.arange(DIL_BLOCK)[:, None]
    j = jnp.arange(2 * DIL_BLOCK)[None, :]
    dist = i + DIL_BLOCK - j
    mask = (dist >= 0) & (dist <= steps)
    first = (jnp.arange(nb)[:, None, None] > 0) | (j[None] >= DIL_BLOCK)
    mask = mask[None] & first
    bias = table[:, t5_bucket(jnp.maximum(dist, 0) * dil)]
    sc = jnp.einsum('bhrnqd,bhrnkd->bhrnqk', qb, kk, preferred_element_type=jnp.float32) * SCALE
    sc = jnp.where(mask, sc + bias[None, :, None, None], -jnp.inf)
    m = jnp.max(sc, axis=-1, keepdims=True)
    p = jnp.exp(sc - m)
    den = jnp.sum(p, axis=-1, keepdims=True)
    o = jnp.einsum('bhrnqk,bhrnkd->bhrnqd', p, vv.astype(jnp.float32)) / den
    lse = (m + jnp.log(den))[..., 0]
    o = o.reshape(b, h, dil, length, dh).transpose(0, 1, 3, 2, 4).reshape(b, h, sp, dh)[:, :, :s]
    lse = lse.reshape(b, h, dil, length).transpose(0, 1, 3, 2).reshape(b, h, sp)[:, :, :s]
    return o, lse


def mixer_dilated(q, k, v, table):
    outs, lses = [], []
    for window, dil in DIL_PATTERNS:
        o, lse = dilated_attention(q, k, v, table, window, dil)
        outs.append(o)
        lses.append(lse)
    w = jax.nn.softmax(jnp.stack(lses, axis=0), axis=0)
    return jnp.sum(w[..., None] * jnp.stack(outs, axis=0), axis=0)


def nsa_compress(kraw, pos, w1, b1, w2):
    b, g, s, dh = kraw.shape
    ch = kraw.reshape(b, g, s // CMP_STRIDE, CMP_STRIDE, dh)
    blk = jnp.concatenate([ch[:, :, :-1], ch[:, :, 1:]], axis=3) + pos
    hid = jax.nn.gelu(jnp.einsum('bgnld,ldh->bgnh', blk, w1) + b1)
    return hid @ w2


def nsa_compressed_branch(q, kc, vc):
    s = q.shape[3]
    nc = kc.shape[2]
    t = jnp.arange(s)
    block_end = jnp.arange(nc) * CMP_STRIDE + CMP_LEN - 1
    mask = block_end[None, :] <= t[:, None]
    sc = jnp.einsum('bgrqd,bgnd->bgrqn', q, kc, preferred_element_type=jnp.float32) * SCALE
    p = masked_softmax(sc, mask)
    o = jnp.einsum('bgrqn,bgnd->bgrqd', p, vc.astype(jnp.float32))
    return o, jnp.sum(p, axis=2)


def block_importance(pc, nb):
    ratio = SEL_BLOCK // CMP_STRIDE
    pp = jnp.pad(pc, ((0, 0), (0, 0), (0, 0), (1, 1)))
    imp = 0.0
    for off, w in zip(range(-1, ratio), CMP_OVERLAP):
        imp = imp + w * pp[..., off + 1: off + 2 + ratio * (nb - 1): ratio]
    return imp


def nsa_selected_branch(q, k, v, imp, table):
    b, g, r, s, dh = q.shape
    nb = s // SEL_BLOCK
    topn = min(SEL_TOPN, nb)
    t = jnp.arange(s)
    cur = (t // SEL_BLOCK)[:, None]
    j = jnp.arange(nb)[None, :]
    forced = (j == 0) | (j == cur) | (j == cur - 1)
    imp = jnp.where(j > cur, -jnp.inf, jnp.where(forced, jnp.inf, imp))
    _, idx = lax.top_k(imp, topn)
    kblk = k.reshape(b, g, nb, SEL_BLOCK * dh)
    vblk = v.reshape(b, g, nb, SEL_BLOCK * dh)
    nch = s // SEL_Q_CHUNK
    qc = q.reshape(b, g, r, nch, SEL_Q_CHUNK, dh).transpose(3, 0, 1, 2, 4, 5)
    ic = idx.reshape(b, g, nch, SEL_Q_CHUNK, topn).transpose(2, 0, 1, 3, 4)
    tab = table.transpose(0, 2, 1)
    b_idx = jnp.arange(b)[:, None, None]
    g_idx3 = jnp.arange(g)[None, :, None]
    g_idx4 = jnp.arange(g)[None, :, None, None]
    nkey = topn * SEL_BLOCK

    def chunk(args):
        qq, ii, c = args
        flat = ii.reshape(b, g, SEL_Q_CHUNK * topn)
        kg = kblk[b_idx, g_idx3, flat].reshape(b, g, SEL_Q_CHUNK, nkey, dh)
        vg = vblk[b_idx, g_idx3, flat].reshape(b, g, SEL_Q_CHUNK, nkey, dh)
        kpos = (ii[..., None] * SEL_BLOCK + jnp.arange(SEL_BLOCK)).reshape(b, g, SEL_Q_CHUNK, nkey)
        tq = c * SEL_Q_CHUNK + jnp.arange(SEL_Q_CHUNK)
        dist = tq[None, None, :, None] - kpos
        bias = tab[g_idx4, t5_bucket(jnp.maximum(dist, 0))].transpose(0, 1, 4, 2, 3)
        sc = jnp.einsum('bgrqd,bgqkd->bgrqk', qq, kg, preferred_element_type=jnp.float32) * SCALE + bias
        p = masked_softmax(sc, (dist >= 0)[:, :, None])
        return jnp.einsum('bgrqk,bgqkd->bgrqd', p, vg.astype(jnp.float32))

    out = lax.map(chunk, (qc, ic, jnp.arange(nch)))
    return out.transpose(1, 2, 3, 0, 4, 5).reshape(b, g, r, s, dh)


def nsa_window_branch(q, k, v, table):
    b, g, r, s, dh = q.shape
    nb = s // WIN_BLOCK
    nw = WIN_LEN // WIN_BLOCK
    qb = q.reshape(b, g, r, nb, WIN_BLOCK, dh)

    def band(t):
        tb = jnp.pad(t.reshape(b, g, nb, WIN_BLOCK, dh), ((0, 0), (0, 0), (nw, 0), (0, 0), (0, 0)))
        return jnp.concatenate([tb[:, :, o:o + nb] for o in range(nw + 1)], axis=3)

    kk, vv = band(k), band(v)
    i = jnp.arange(WIN_BLOCK)[:, None]
    j = jnp.arange((nw + 1) * WIN_BLOCK)[None, :]
    dist = i + nw * WIN_BLOCK - j
    kpos = jnp.arange(nb)[:, None, None] * WIN_BLOCK + j[None] - nw * WIN_BLOCK
    mask = ((dist >= 0) & (dist < WIN_LEN))[None] & (kpos >= 0)
    bias = table[:, :, t5_bucket(jnp.maximum(dist, 0))]
    sc = jnp.einsum('bgrnqd,bgnkd->bgrnqk', qb, kk, preferred_element_type=jnp.float32) * SCALE
    p = masked_softmax(sc + bias[None, :, :, None], mask)
    o = jnp.einsum('bgrnqk,bgnkd->bgrnqd', p, vv.astype(jnp.float32))
    return o.reshape(b, g, r, s, dh)


def setup_inputs(seed: int = 0) -> dict:
    key = jax.random.key(seed)
    ks = jax.random.split(key, 18)
    f32 = jnp.float32
    nrm = lambda k, shape, scale: jax.random.normal(k, shape, f32) * scale
    return {
        "x": nrm(ks[0], (BATCH, SEQ, D_MODEL), 1.0),
        "norm1_g": 1.0 + nrm(ks[1], (DEPTH, D_MODEL), 0.02),
        "w_in": nrm(ks[2], (DEPTH, D_MODEL, IN_WIDTH), D_MODEL ** -0.5),
        "rel_bias": nrm(ks[3], (REL_BUCKETS, N_HEADS_TOTAL), 0.1),
        "cmp_pos": nrm(ks[4], (DEPTH, CMP_LEN, HEAD_DIM), 0.2),
        "cmp_k_w1": nrm(ks[5], (DEPTH, CMP_LEN, HEAD_DIM, CMP_HIDDEN), (CMP_LEN * HEAD_DIM) ** -0.5),
        "cmp_k_b1": nrm(ks[6], (DEPTH, CMP_HIDDEN), 0.02),
        "cmp_k_w2": nrm(ks[7], (DEPTH, CMP_HIDDEN, HEAD_DIM), CMP_HIDDEN ** -0.5),
        "cmp_v_w1": nrm(ks[8], (DEPTH, CMP_LEN, HEAD_DIM, CMP_HIDDEN), (CMP_LEN * HEAD_DIM) ** -0.5),
        "cmp_v_b1": nrm(ks[9], (DEPTH, CMP_HIDDEN), 0.02),
        "cmp_v_w2": nrm(ks[10], (DEPTH, CMP_HIDDEN, HEAD_DIM), CMP_HIDDEN ** -0.5),
        "w_out": nrm(ks[11], (DEPTH, A_WIDTH + B_WIDTH, D_MODEL), (A_WIDTH + B_WIDTH) ** -0.5),
        "norm2_g": 1.0 + nrm(ks[12], (DEPTH, D_MODEL), 0.02),
        "w_gate": nrm(ks[13], (DEPTH, D_MODEL, FFN_HIDDEN), D_MODEL ** -0.5),
        "w_up": nrm(ks[14], (DEPTH, D_MODEL, FFN_HIDDEN), D_MODEL ** -0.5),
        "w_down": nrm(ks[15], (DEPTH, FFN_HIDDEN, D_MODEL), FFN_HIDDEN ** -0.5),
        "norm_f_g": 1.0 + nrm(ks[16], (D_MODEL,), 0.02),
    }


def reference(x, norm1_g, w_in, rel_bias, cmp_pos, cmp_k_w1, cmp_k_b1, cmp_k_w2,
              cmp_v_w1, cmp_v_b1, cmp_v_w2, w_out, norm2_g, w_gate, w_up, w_down, norm_f_g):
    b, s, _ = x.shape
    tab_a = rel_bias[:, :A_HEADS].T
    tab_b = rel_bias[:, A_HEADS:].T.reshape(B_KV_HEADS, B_GROUP, REL_BUCKETS)
    h = x
    for layer in range(DEPTH):
        xn = rms_norm(h, norm1_g[layer])
        aq, ak, av, bq, kc, vc, ksl, vsl, kw, vw, gl = split_columns(xn @ w_in[layer])
        o_a = mixer_dilated(to_heads(aq, A_HEADS), to_heads(ak, A_HEADS), to_heads(av, A_HEADS), tab_a)
        q_b = to_heads(bq, B_HEADS).reshape(b, B_KV_HEADS, B_GROUP, s, HEAD_DIM)
        k_cmp = nsa_compress(to_heads(kc, B_KV_HEADS), cmp_pos[layer], cmp_k_w1[layer], cmp_k_b1[layer], cmp_k_w2[layer])
        v_cmp = nsa_compress(to_heads(vc, B_KV_HEADS), cmp_pos[layer], cmp_v_w1[layer], cmp_v_b1[layer], cmp_v_w2[layer])
        o_cmp, p_cmp = nsa_compressed_branch(q_b, k_cmp, v_cmp)
        imp = block_importance(p_cmp, s // SEL_BLOCK)
        o_slc = nsa_selected_branch(q_b, to_heads(ksl, B_KV_HEADS), to_heads(vsl, B_KV_HEADS), imp, tab_b)
        o_win = nsa_window_branch(q_b, to_heads(kw, B_KV_HEADS), to_heads(vw, B_KV_HEADS), tab_b)
        gate = jax.nn.sigmoid(gl.astype(jnp.float32)).reshape(b, s, B_KV_HEADS, B_GROUP, N_BRANCH).transpose(0, 2, 3, 1, 4)
        o_b = gate[..., 0:1] * o_cmp + gate[..., 1:2] * o_slc + gate[..., 2:3] * o_win
        mix = jnp.concatenate([
            o_a.transpose(0, 2, 1, 3).reshape(b, s, A_WIDTH),
            o_b.reshape(b, B_HEADS, s, HEAD_DIM).transpose(0, 2, 1, 3).reshape(b, s, B_WIDTH),
        ], axis=-1).astype(h.dtype)
        h = h + mix @ w_out[layer]
        hn = rms_norm(h, norm2_g[layer])
        h = h + (jax.nn.silu(hn @ w_gate[layer]) * (hn @ w_up[layer])) @ w_down[layer]
    return rms_norm(h, norm_f_g)
```

```python
import contextlib
import math
import types
import numpy as np
import ml_dtypes
import concourse.bass as bass
import concourse.mybir as mybir
from concourse.bass_utils import run_bass_kernel_spmd

F32 = mybir.dt.float32
BF16 = mybir.dt.bfloat16
ALU = mybir.AluOpType
AF = mybir.ActivationFunctionType

SEQ = 4096
D = 1024
NT = SEQ // 128
HD = 64
FFN = 2816
NHC = FFN // 128
IN_W = 2840
REL_BUCKETS = 32
NEG = -30000.0
EPS = 1e-6
DIL = ((128, 1), (512, 4), (2048, 16))

RAW, WAW, WAR = 1, 2, 4


class TT:
    __slots__ = ("name", "w", "r", "sem")

    def __init__(self, name):
        self.name = name
        self.w = None
        self.r = {}
        self.sem = None


class Op:
    __slots__ = ("eng", "fn", "deps", "dma", "semkey", "pos", "val", "signal", "waits")


class Sched:
    ENGS = ("pe", "act", "dve", "pool", "sp")

    def __init__(self, nc):
        self.nc = nc
        self.ops = []
        self.eobj = {"pe": nc.tensor, "act": nc.scalar, "dve": nc.vector, "pool": nc.gpsimd, "sp": nc.sync}
        self.ndsem = 0
        self.all_tt = []

    def tt(self, name):
        t = TT(name)
        self.all_tt.append(t)
        return t

    @staticmethod
    def _freeze(fn):
        if fn is None or fn.__closure__ is None:
            return fn
        cells = []
        for c in fn.__closure__:
            try:
                v = c.cell_contents
            except ValueError:
                cells.append(c)
                continue
            if isinstance(v, types.FunctionType) and v.__closure__ is not None and v is not fn:
                v = Sched._freeze(v)
            cells.append(types.CellType(v))
        return types.FunctionType(fn.__code__, fn.__globals__, fn.__name__, fn.__defaults__, tuple(cells))

    def add(self, eng, fn, reads=(), writes=(), dma=False):
        fn = self._freeze(fn)
        op = Op()
        idx = len(self.ops)
        op.eng, op.fn, op.dma = eng, fn, dma
        op.signal = False
        deps = {}
        for t in reads:
            if t.w is not None:
                deps[t.w] = deps.get(t.w, 0) | RAW
        for t in writes:
            if t.w is not None:
                deps[t.w] = deps.get(t.w, 0) | WAW
            for ri in t.r.values():
                deps[ri] = deps.get(ri, 0) | WAR
        deps.pop(idx, None)
        op.deps = deps
        if dma:
            t0 = writes[0]
            if t0.sem is None:
                t0.sem = self.ndsem
                self.ndsem += 1
            op.semkey = ("d", t0.sem)
        else:
            op.semkey = eng
        rkey = op.semkey
        for t in reads:
            t.r[rkey] = idx
        for t in writes:
            t.w = idx
            t.r = {}
        self.ops.append(op)
        return idx

    def dma(self, eng, out, in_, reads, writes, **kw):
        return self.add(eng, lambda e: e.dma_start(out=out, in_=in_, **kw), reads, writes, dma=True)

    def barrier(self):
        deps = {}
        for t in self.all_tt:
            if t.w is not None:
                deps[t.w] = RAW | WAW | WAR
            for ri in t.r.values():
                deps[ri] = RAW | WAW | WAR
        for k in self.ENGS:
            op = Op()
            op.eng, op.fn, op.dma, op.signal = k, None, False, False
            op.deps = dict(deps)
            op.semkey = k
            self.ops.append(op)

    def emit(self, stack, verbose=False):
        ops = self.ops
        pos = {k: 0 for k in self.ENGS}
        dcount = {}
        clock = {k: {} for k in self.ENGS}
        done_clock = [None] * len(ops)
        for i, op in enumerate(ops):
            E = op.eng
            ck = clock[E]
            waits = {}
            for d, kind in op.deps.items():
                dop = ops[d]
                if dop.fn is None:
                    continue
                key = dop.semkey
                if (not dop.dma) and dop.eng == E and (not op.dma) and op.fn is not None:
                    if E == "pe":
                        continue
                if ck.get(key, 0) >= dop.pos:
                    continue
                if waits.get(key, (0, 0))[0] < dop.pos:
                    waits[key] = (dop.pos, d)
            for key, (p, d) in waits.items():
                ops[d].signal = True
                for k2, v2 in done_clock[d].items():
                    if ck.get(k2, 0) < v2:
                        ck[k2] = v2
            op.waits = [(key, d) for key, (p, d) in waits.items()]
            if op.fn is None:
                op.pos = 0
                done_clock[i] = None
                continue
            if op.dma:
                sk = op.semkey
                dcount[sk] = dcount.get(sk, 0) + 1
                op.pos = dcount[sk]
            else:
                pos[E] += 1
                op.pos = pos[E]
            snap = dict(ck)
            snap[op.semkey] = op.pos
            done_clock[i] = snap
        cnt = {k: 0 for k in self.ENGS}
        for op in ops:
            if op.fn is None:
                continue
            if op.dma:
                op.val = 16 * op.pos
            else:
                if op.signal:
                    cnt[op.eng] += 1
                op.val = cnt[op.eng]
        nc = self.nc
        esem = {k: stack.enter_context(nc.semaphore("s_" + k)) for k in self.ENGS}
        dsem = [stack.enter_context(nc.semaphore("d%d" % j)) for j in range(self.ndsem)]
        nw = 0
        for op in ops:
            e = self.eobj[op.eng]
            for key, d in op.waits:
                sem = esem[key] if isinstance(key, str) else dsem[key[1]]
                e.wait_ge(sem, ops[d].val)
                nw += 1
            if op.fn is None:
                continue
            ins = op.fn(e)
            if op.dma:
                ins.then_inc(dsem[op.semkey[1]], 16)
            elif op.signal:
                ins.then_inc(esem[op.eng], 1)
        if verbose:
            print("[sched] ops", len(ops), "waits", nw, "signals", cnt, "dma sems", self.ndsem, flush=True)


def sl(start, count, step=1):
    return slice(start, start + (count - 1) * step + 1, step)


class Rot:
    def __init__(self, items):
        self.items = items
        self.i = 0

    def next(self):
        it = self.items[self.i % len(self.items)]
        self.i += 1
        return it


def _t5_bucket_np(dist):
    dist = np.asarray(dist, np.int64)
    max_exact = REL_BUCKETS // 2
    df = np.maximum(dist, 1).astype(np.float32)
    large = max_exact + (np.log(df / np.float32(max_exact)) / np.float32(math.log(2048 / max_exact))
                         * np.float32(REL_BUCKETS - max_exact)).astype(np.int32)
    large = np.minimum(large, REL_BUCKETS - 1)
    return np.where(dist < max_exact, dist, large)


LA = 256 + 127
LS = 2560 + 127
LW = 1408 + 127
OH_OFF = {"a0": 0, "a1": LA, "a2": 2 * LA, "s": 3 * LA, "w": 3 * LA + LS}
OH_LEN = 3 * LA + LS + LW


def _onehot_const():
    oh = np.zeros((33, OH_LEN), np.float32)

    def fill(off, L, dist_of_i, valid_of_i):
        i = np.arange(L)
        dist = dist_of_i(i)
        valid = valid_of_i(dist)
        b = _t5_bucket_np(np.maximum(dist, 0))
        for j in range(L):
            if valid[j]:
                oh[b[j], off + j] = 1.0
            else:
                oh[32, off + j] = 1.0

    for p, (win, dil) in enumerate(DIL):
        fill(OH_OFF["a%d" % p], LA, lambda i, dil=dil: (i - 127) * dil, lambda d, dil=dil: (d >= 0) & (d <= 128 * dil))
    fill(OH_OFF["s"], LS, lambda i: i - 127 - 384, lambda d: d >= 0)
    fill(OH_OFF["w"], LW, lambda i: i - 127 - 384, lambda d: (d >= 0) & (d < 512))
    return oh


def build_consts():
    c = {}
    c["c_oh"] = _onehot_const()
    c["c_ident_bf"] = np.eye(128, dtype=np.float32).astype(ml_dtypes.bfloat16)
    c["c_ident_f"] = np.eye(128, dtype=np.float32)
    bf = ml_dtypes.bfloat16
    key = np.arange(SEQ)
    c["c_ex"] = (key[None, :] // 64 == np.arange(64)[:, None]).astype(np.float32).astype(bf)
    n = np.arange(128)[:, None]
    xx = np.arange(2560)[None, :]
    c["c_mc"] = ((xx - 16 * n - 31) >= 0).astype(np.float32).astype(bf)
    mimp = np.zeros((256, 128), np.float32)
    for j in range(64):
        for off, w in zip(range(-1, 4), (1.0, 2.0, 2.0, 2.0, 1.0)):
            nn = 4 * j + off
            if 0 <= nn <= 254:
                mimp[nn, j] = w
    c["c_mimp"] = np.ascontiguousarray(mimp.reshape(2, 128, 128).transpose(1, 0, 2)).astype(bf)
    keep = np.zeros((128, 126), np.float32)
    add = np.zeros((128, 126), np.float32)
    for p in range(128):
        for y in range(126):
            delta = y - 62 - (1 if p >= 64 else 0)
            if delta > 0:
                add[p, y] = -1e30
            elif delta == 0:
                add[p, y] = 3e30
            elif delta == -1:
                add[p, y] = 2e30
            else:
                keep[p, y] = 1.0
    c["c_keep"] = keep
    c["c_add"] = add
    sel = np.zeros((128, 12, 128), np.float32)
    for r in range(12):
        sel[r, r, :] = 1.0
    c["c_sel"] = sel
    return c


A_STOP = None


def build_program(parts=("A", "B"), debug=()):
    nc = bass.Bass("TRN2", target_bir_lowering=False)
    S = Sched(nc)

    def din(name, shape, dt=F32):
        return nc.dram_tensor(name, list(shape), dt, kind="ExternalInput").ap()

    x_d = din("x", [SEQ, D])
    g1_d = din("norm1_g", [1, D])
    w_in_d = din("w_in", [D, IN_W])
    relb_d = din("rel_bias", [REL_BUCKETS, 16])
    pos_d = din("cmp_pos", [32, HD])
    kw1_d = din("cmp_k_w1", [32, HD, 128])
    kb1_d = din("cmp_k_b1", [1, 128])
    kw2_d = din("cmp_k_w2", [128, HD])
    vw1_d = din("cmp_v_w1", [32, HD, 128])
    vb1_d = din("cmp_v_b1", [1, 128])
    vw2_d = din("cmp_v_w2", [128, HD])
    w_out_d = din("w_out", [D, D])
    g2_d = din("norm2_g", [1, D])
    w_gate_d = din("w_gate", [D, FFN])
    w_up_d = din("w_up", [D, FFN])
    w_down_d = din("w_down", [FFN, D])
    gf_d = din("norm_f_g", [1, D])
    c_oh_d = din("c_oh", [33, OH_LEN])
    c_identb_d = din("c_ident_bf", [128, 128], BF16)
    c_identf_d = din("c_ident_f", [128, 128])
    c_ex_d = din("c_ex", [64, SEQ], BF16)
    c_mc_d = din("c_mc", [128, 2560], BF16)
    c_mimp_d = din("c_mimp", [128, 2, 128], BF16)
    c_keep_d = din("c_keep", [128, 126])
    c_add_d = din("c_add", [128, 126])
    c_sel_d = din("c_sel", [128, 12, 128])
    y_d = nc.dram_tensor("y", [SEQ, D], F32, kind="ExternalOutput").ap()
    dbg_d = {}
    for name, shape, dt in debug:
        dbg_d[name] = nc.dram_tensor("dbg_" + name, list(shape), dt, kind="ExternalOutput").ap()

    xnT_d = nc.dram_tensor("xnT_scr", [D, SEQ], BF16, kind="Internal").ap()
    mixT_d = nc.dram_tensor("mixT_scr", [D, SEQ], BF16, kind="Internal").ap()
    T_xnT_d = S.tt("xnT_d")
    T_mixT_d = [S.tt("mixT_d%d" % j) for j in range(8)]
    T_in = S.tt("inputs")
    T_y = S.tt("y")

    outer = contextlib.ExitStack()
    with outer:
        uniq = [0]

        def sb(stack, name, shape, dt):
            uniq[0] += 1
            return stack.enter_context(nc.sbuf_tensor("%s_%d" % (name, uniq[0]), list(shape), dt))

        def ps(stack, name, shape, dt):
            uniq[0] += 1
            return stack.enter_context(nc.psum_tensor("%s_%d" % (name, uniq[0]), list(shape), dt))

        identb = sb(outer, "identb", [128, 128], BF16); T_identb = S.tt("identb")
        identf = sb(outer, "identf", [128, 128], F32); T_identf = S.tt("identf")
        g1T = sb(outer, "g1T", [128, 8], F32); T_g1T = S.tt("g1T")
        g2T = sb(outer, "g2T", [128, 8], F32); T_g2T = S.tt("g2T")
        S.dma("sp", identb[:], c_identb_d[:, :], [T_in], [T_identb])
        S.dma("sp", identf[:], c_identf_d[:, :], [T_in], [T_identf])
        S.dma("sp", g1T[:], g1_d.rearrange("o (j p) -> p (o j)", p=128), [T_in], [T_g1T],
              allow_slow_non_contiguous=True)
        S.dma("sp", g2T[:], g2_d.rearrange("o (j p) -> p (o j)", p=128), [T_in], [T_g2T],
              allow_slow_non_contiguous=True)

        tab = sb(outer, "tab", [33, 16], F32); T_tab = S.tt("tab")
        S.dma("sp", tab[0:32, :], relb_d[:, :], [T_in], [T_tab])
        S.add("dve", lambda e: e.memset(tab[32:33, :], NEG), [], [T_tab])
        zer33 = sb(outer, "zer33", [33, 128], F32); T_zer33 = S.tt("zer33")
        S.add("dve", lambda e: e.memset(zer33[:, :], 0.0), [], [T_zer33])

        ph1 = contextlib.ExitStack()
        xnT = sb(ph1, "xnT", [128, 8, SEQ], BF16)
        T_xnT = [S.tt("xnT_t%d" % i) for i in range(NT)]
        with contextlib.ExitStack() as p0:
            xts = Rot([(sb(p0, "xt%d" % i, [128, D], F32), S.tt("xt%d" % i)) for i in range(3)])
            junk = sb(p0, "junk", [128, D], BF16); T_junk = S.tt("junk")
            xnbs = Rot([(sb(p0, "xnb%d" % i, [128, D], BF16), S.tt("xnb%d" % i)) for i in range(2)])
            ss = sb(p0, "ss", [128, NT], F32); T_ss = [S.tt("ss%d" % i) for i in range(NT)]
            sq = sb(p0, "sq", [128, NT], F32); T_sq = [S.tt("sq%d" % i) for i in range(NT)]
            rs = sb(p0, "rs", [128, NT], F32); T_rs = [S.tt("rs%d" % i) for i in range(NT)]
            ptr = Rot([(ps(p0, "ptr%d" % i, [128, 8, 128], BF16), S.tt("ptr%d" % i)) for i in range(2)])
            for i in range(NT):
                xt, T_xt = xts.next()
                S.dma("sp", xt[:], x_d[i * 128:(i + 1) * 128, :], [T_in], [T_xt])
                S.add("act", lambda e, xt=xt, i=i: e.activation(out=junk[:], in_=xt[:], func=AF.Square,
                                                                 accum_out=ss[:, i:i + 1]),
                      [T_xt], [T_junk, T_ss[i]])
                S.add("act", lambda e, i=i: e.activation(out=sq[:, i:i + 1], in_=ss[:, i:i + 1], func=AF.Sqrt,
                                                         scale=1.0 / D, bias=EPS),
                      [T_ss[i]], [T_sq[i]])
                S.add("dve", lambda e, i=i: e.reciprocal(out=rs[:, i:i + 1], in_=sq[:, i:i + 1]), [T_sq[i]], [T_rs[i]])
                xnb, T_xnb = xnbs.next()
                S.add("dve", lambda e, xt=xt, xnb=xnb, i=i: e.tensor_scalar(xnb[:], xt[:], rs[:, i:i + 1], None, ALU.mult),
                      [T_xt, T_rs[i]], [T_xnb])
                pt, T_pt = ptr.next()
                for j in range(8):
                    S.add("pe", lambda e, pt=pt, xnb=xnb, j=j: e.transpose(pt[:, j, :], xnb[:, j * 128:(j + 1) * 128], identb[:]),
                          [T_xnb, T_identb], [T_pt])
                S.add("act", lambda e, pt=pt, i=i: e.activation(out=xnT[:, :, i * 128:(i + 1) * 128], in_=pt[:], func=AF.Copy),
                      [T_pt], [T_xnT[i]])
            for j in range(8):
                S.dma("pool", xnT_d[j * 128:(j + 1) * 128, :], xnT[:, j, :], T_xnT, [T_xnT_d])
            if "xnT" in dbg_d:
                for j in range(8):
                    S.dma("pool", dbg_d["xnT"][j * 128:(j + 1) * 128, :], xnT[:, j, :], T_xnT, [T_y])
            S.barrier()

        w_in_v = w_in_d.rearrange("(j p) n -> p j n", p=128)
        repA_d = nc.dram_tensor("repA_scr", [24, 128, LA], F32, kind="Internal").ap()

        def load_w_in(stack_bufs, c0, ncols, dst, T_dst, dcol=0, xsrc=None):
            stg, T_stg = stack_bufs.next()
            S.dma("sp", stg[:, :, 0:ncols], w_in_v[:, :, c0:c0 + ncols], [T_in], [T_stg])
            for j in range(8):
                if j % 2 == 0:
                    S.add("dve", lambda e, j=j: e.tensor_scalar(dst[:, j, dcol:dcol + ncols], stg[:, j, 0:ncols], g1T[:, j:j + 1], None, ALU.mult),
                          [T_stg, T_g1T], [T_dst])
                else:
                    S.add("act", lambda e, j=j: e.activation(out=dst[:, j, dcol:dcol + ncols], in_=stg[:, j, 0:ncols], func=AF.Copy, scale=g1T[:, j:j + 1]),
                          [T_stg, T_g1T], [T_dst])

        def proj512(pp, T_pp, wb, T_wb, c, xn=None, T_xn=None):
            xn = xnT if xn is None else xn
            T_xn = T_xnT[4 * c:4 * c + 4] if T_xn is None else T_xn
            for j in range(8):
                S.add("pe", lambda e, j=j: e.matmul(pp[:], wb[:, j, :], xn[:, j, c * 512:(c + 1) * 512],
                                                     start=(j == 0), stop=(j == 7)),
                      [T_wb] + T_xn, [T_pp])


        repS_d = nc.dram_tensor("repS_scr", [8, 128, LS], F32, kind="Internal").ap()
        repW_d = nc.dram_tensor("repW_scr", [8, 128, LW], F32, kind="Internal").ap()
        w1_v = {"k": kw1_d.rearrange("l d h -> d l h"), "v": vw1_d.rearrange("l d h -> d l h")}

        def build_B():
          with contextlib.ExitStack() as pb:
            QM = [sb(pb, "QM%d" % r, [128, SEQ], BF16) for r in range(4)]
            T_QMq = [S.tt("QMq%d" % r) for r in range(4)]
            T_QMn = [[S.tt("QMn%d_%d" % (r, c)) for c in range(8)] for r in range(4)]
            kslE = sb(pb, "kslE", [128, SEQ], BF16); T_kslE = S.tt("kslE")
            kwz = sb(pb, "kwz", [128, SEQ], BF16); T_kwz = S.tt("kwz")
            vsl_aug = sb(pb, "vsl_aug", [128, 32, 128], BF16); T_vsl = S.tt("vsl_aug")
            vw_aug = sb(pb, "vw_aug", [128, 32, 128], BF16); T_vw = S.tt("vw_aug")
            kcmpz = sb(pb, "kcmpz", [128, 256], BF16); T_kcmpz = S.tt("kcmpz")
            vcmp_aug = sb(pb, "vcmp_aug", [128, 2, 128], BF16); T_vcmp = S.tt("vcmp_aug")
            gT = sb(pb, "gT", [128, SEQ], F32); T_gT = S.tt("gT")
            MC = sb(pb, "MC", [128, 2560], BF16); T_MC = S.tt("MC")
            selc = sb(pb, "selc", [128, 12, 128], F32); T_selc = S.tt("selc")
            keepS = sb(pb, "keepS", [128, 126], F32); T_keepS = S.tt("keepS")
            addS = sb(pb, "addS", [128, 126], F32); T_addS = S.tt("addS")
            mimp = sb(pb, "mimp", [128, 2, 128], BF16); T_mimp = S.tt("mimp")
            S.dma("sp", MC[:], c_mc_d[:, :], [T_in], [T_MC])
            S.dma("sp", selc[:], c_sel_d[:, :, :], [T_in], [T_selc])
            S.dma("sp", keepS[:], c_keep_d[:, :], [T_in], [T_keepS])
            S.dma("sp", addS[:], c_add_d[:, :], [T_in], [T_addS])
            S.dma("sp", mimp[:], c_mimp_d[:, :, :], [T_in], [T_mimp])
            S.dma("sp", kslE[64:128, :], c_ex_d[:, :], [T_in], [T_kslE])
            S.add("dve", lambda e: e.memset(kwz[64:128, :], 0.0), [], [T_kwz])
            for r in range(4):
                S.add("dve", lambda e, r=r: e.memset(QM[r][64:128, :], 0.0), [], T_QMn[r])
            S.add("dve", lambda e: e.memset(vsl_aug[:, :, 0:64], 1.0), [], [T_vsl])
            S.add("dve", lambda e: e.memset(vw_aug[:, :, 0:64], 1.0), [], [T_vw])
            S.add("dve", lambda e: e.memset(vcmp_aug[:, :, 0:64], 1.0), [], [T_vcmp])

            for g in range(2):
                with contextlib.ExitStack() as pj:
                    xnb_ = sb(pj, "xnTb", [128, 8, SEQ], BF16); T_xnb = S.tt("xnTb%d" % g)
                    for j in range(8):
                        S.dma("sp", xnb_[:, j, :], xnT_d[j * 128:(j + 1) * 128, :], [T_xnT_d], [T_xnb])
                    wst = Rot([(sb(pj, "wstB", [128, 8, 128], F32), S.tt("wstB%d" % g))])
                    wbs = Rot([(sb(pj, "wbB%d" % i, [128, 8, 128], BF16), S.tt("wbB%d_%d" % (g, i))) for i in range(2)])
                    tmpT = sb(pj, "tmpT", [128, SEQ], BF16); T_tmpT = S.tt("tmpT%d" % g)
                    w1z = sb(pj, "w1z", [128, 32, 128], BF16); T_w1z = S.tt("w1z%d" % g)
                    w2t = sb(pj, "w2t", [128, 128], BF16); T_w2t = S.tt("w2t%d" % g)
                    w2s = sb(pj, "w2s", [128, 64], F32); T_w2s = S.tt("w2s%d" % g)
                    posT2 = sb(pj, "posT2", [128, 32], BF16); T_posT2 = S.tt("posT2_%d" % g)
                    poss = sb(pj, "poss", [128, 32], F32); T_poss = S.tt("poss%d" % g)
                    b1c = sb(pj, "b1c", [128, 1], F32); T_b1c = S.tt("b1c%d" % g)
                    cb = sb(pj, "cb", [128, 1], F32); T_cb = S.tt("cb%d" % g)
                    zt = sb(pj, "zt", [128, 256], F32); T_zt = S.tt("zt%d" % g)
                    ut = sb(pj, "ut", [128, 256], F32); T_ut = S.tt("ut%d" % g)
                    hidT = sb(pj, "hidT", [128, 256], BF16); T_hidT = S.tt("hidT%d" % g)
                    vtmp = Rot([(sb(pj, "vtmpB%d" % i, [128, 8, 128], BF16), S.tt("vtmpB%d_%d" % (g, i))) for i in range(2)])
                    ppj = Rot([(ps(pj, "ppjB%d" % i, [128, 512], F32), S.tt("ppjB%d_%d" % (g, i))) for i in range(2)])
                    phid = ps(pj, "phid", [128, 256], F32); T_phid = S.tt("phid%d" % g)
                    pcb = ps(pj, "pcb", [128, 8], F32); T_pcb = S.tt("pcb%d" % g)
                    ptrB = ps(pj, "ptrB", [128, 8, 128], BF16); T_ptrB = S.tt("ptrB%d" % g)
                    Txc = lambda c: [T_xnb]

                    def proj_pair(colspecs, evac):
                        wb, T_wb = wbs.next()
                        if sum(n_ for _, n_, _ in colspecs) < 128:
                            S.add("dve", lambda e, wb=wb: e.memset(wb[:], 0.0), [], [T_wb])
                        for c0, n_, dcol in colspecs:
                            load_w_in(wst, c0, n_, wb, T_wb, dcol=dcol)
                        for c in range(8):
                            pp, T_pp = ppj.next()
                            proj512(pp, T_pp, wb, T_wb, c, xn=xnb_, T_xn=[T_xnb])
                            evac(pp, T_pp, c)

                    for pair in range(2):
                        def evq(pp, T_pp, c, pair=pair):
                            S.add("act", lambda e: e.activation(out=QM[2 * pair][0:64, c * 512:(c + 1) * 512], in_=pp[0:64, :],
                                                                func=AF.Copy, scale=0.125), [T_pp], [T_QMq[2 * pair]])
                            S.add("dve", lambda e: e.tensor_scalar(QM[2 * pair + 1][0:64, c * 512:(c + 1) * 512], pp[64:128, :], 0.125, None, ALU.mult),
                                  [T_pp], [T_QMq[2 * pair + 1]])
                        proj_pair([(1536 + g * 256 + pair * 128, 128, 0)], evq)
                    def evk(pp, T_pp, c):
                        S.add("act", lambda e: e.activation(out=kslE[0:64, c * 512:(c + 1) * 512], in_=pp[0:64, :], func=AF.Copy), [T_pp], [T_kslE])
                        S.add("dve", lambda e: e.tensor_copy(kwz[0:64, c * 512:(c + 1) * 512], pp[64:128, :]), [T_pp], [T_kwz])
                    proj_pair([(2304 + g * 64, 64, 0), (2560 + g * 64, 64, 64)], evk)
                    def evg(pp, T_pp, c):
                        S.add("act", lambda e: e.activation(out=gT[:, c * 512:(c + 1) * 512], in_=pp[:], func=AF.Tanh, scale=0.5), [T_pp], [T_gT])
                        S.add("dve", lambda e: e.tensor_scalar(gT[:, c * 512:(c + 1) * 512], gT[:, c * 512:(c + 1) * 512], 0.5, 0.5, ALU.mult, ALU.add),
                              [T_gT], [T_gT])
                    proj_pair([(2816 + g * 12, 12, 0)], evg)
                    def evv(pp, T_pp, c):
                        S.add("act", lambda e: e.activation(out=tmpT[:, c * 512:(c + 1) * 512], in_=pp[:], func=AF.Copy), [T_pp], [T_tmpT])
                    proj_pair([(2432 + g * 64, 64, 0), (2688 + g * 64, 64, 64)], evv)
                    for m0 in range(0, 32, 8):
                        for mm in range(8):
                            m = m0 + mm
                            S.add("pe", lambda e, mm=mm, m=m: e.transpose(ptrB[:, mm, :], tmpT[:, m * 128:(m + 1) * 128], identb[:]),
                                  [T_tmpT, T_identb], [T_ptrB])
                        vt, T_vt = vtmp.next()
                        S.add("act", lambda e, vt=vt: e.activation(out=vt[:], in_=ptrB[:], func=AF.Copy), [T_ptrB], [T_vt])
                        S.add("dve", lambda e, vt=vt, m0=m0: e.tensor_copy(vsl_aug[:, m0:m0 + 8, 64:128], vt[:, :, 0:64]), [T_vt], [T_vsl])
                        S.add("dve", lambda e, vt=vt, m0=m0: e.tensor_copy(vw_aug[:, m0:m0 + 8, 64:128], vt[:, :, 64:128]), [T_vt], [T_vw])
                    proj_pair([(2048 + g * 64, 64, 0), (2176 + g * 64, 64, 64)], evv)
                    for which, w1v_, b1_d, w2_d, rows in (("k", w1_v["k"], kb1_d, kw2_d, slice(0, 64)), ("v", w1_v["v"], vb1_d, vw2_d, slice(64, 128))):
                        S.add("dve", lambda e: e.memset(w1z[:], 0.0), [], [T_w1z])
                        for l0 in range(0, 32, 8):
                            stg, T_stg = wst.next()
                            S.dma("sp", stg[rows, :, :], w1v_[:, l0:l0 + 8, :], [T_in], [T_stg])
                            S.add("dve", lambda e, stg=stg, l0=l0, rows=rows: e.tensor_copy(w1z[rows, l0:l0 + 8, :], stg[rows, :, :]), [T_stg], [T_w1z])
                        S.dma("sp", poss[rows, :], pos_d.rearrange("l d -> d l"), [T_in], [T_poss], allow_slow_non_contiguous=True)
                        S.add("dve", lambda e: e.memset(posT2[:], 0.0), [], [T_posT2])
                        S.add("dve", lambda e, rows=rows: e.tensor_copy(posT2[rows, :], poss[rows, :]), [T_poss], [T_posT2])
                        S.dma("sp", b1c[:], b1_d.rearrange("o h -> h o"), [T_in], [T_b1c], allow_slow_non_contiguous=True)
                        for l in range(32):
                            S.add("pe", lambda e, l=l: e.matmul(pcb[:, 0:1], w1z[:, l, :], posT2[:, l:l + 1], start=(l == 0), stop=(l == 31)),
                                  [T_w1z, T_posT2], [T_pcb])
                        S.add("dve", lambda e: e.tensor_tensor(cb[:], pcb[:, 0:1], b1c[:], ALU.add), [T_pcb, T_b1c], [T_cb])
                        for l in range(32):
                            S.add("pe", lambda e, l=l: e.matmul(phid[:, 0:255], w1z[:, l, :], tmpT[:, sl(l, 255, 16)], start=(l == 0), stop=(l == 31)),
                                  [T_w1z, T_tmpT], [T_phid])
                        S.add("act", lambda e: e.activation(out=zt[:, 0:255], in_=phid[:, 0:255], func=AF.Identity, bias=cb[:]), [T_phid, T_cb], [T_zt])
                        S.add("dve", lambda e: e.tensor_tensor(ut[:, 0:255], zt[:, 0:255], zt[:, 0:255], ALU.mult), [T_zt], [T_ut])
                        S.add("dve", lambda e: e.tensor_scalar(ut[:, 0:255], ut[:, 0:255], 0.044715, 1.0, ALU.mult, ALU.add), [T_ut], [T_ut])
                        S.add("dve", lambda e: e.tensor_tensor(ut[:, 0:255], ut[:, 0:255], zt[:, 0:255], ALU.mult), [T_ut, T_zt], [T_ut])
                        S.add("act", lambda e: e.activation(out=ut[:, 0:255], in_=ut[:, 0:255], func=AF.Tanh, scale=0.7978845608028654), [T_ut], [T_ut])
                        S.add("dve", lambda e: e.tensor_scalar(ut[:, 0:255], ut[:, 0:255], 0.5, 0.5, ALU.mult, ALU.add), [T_ut], [T_ut])
                        S.add("dve", lambda e: e.memset(hidT[:], 0.0), [], [T_hidT])
                        S.add("dve", lambda e: e.tensor_tensor(hidT[:, 0:255], ut[:, 0:255], zt[:, 0:255], ALU.mult), [T_ut, T_zt], [T_hidT])
                        S.dma("sp", w2s[:], w2_d[:, :], [T_in], [T_w2s])
                        S.add("dve", lambda e: e.memset(w2t[:], 0.0), [], [T_w2t])
                        if which == "k":
                            S.add("dve", lambda e: e.tensor_copy(w2t[:, 0:64], w2s[:]), [T_w2s], [T_w2t])
                            S.add("pe", lambda e: e.matmul(phid[:, 0:256], w2t[:], hidT[:], start=True, stop=True), [T_w2t, T_hidT], [T_phid])
                            S.add("act", lambda e: e.activation(out=kcmpz[:], in_=phid[:, 0:256], func=AF.Copy), [T_phid], [T_kcmpz])
                        else:
                            S.add("dve", lambda e: e.tensor_copy(w2t[:, 64:128], w2s[:]), [T_w2s], [T_w2t])
                            for t in range(2):
                                S.add("pe", lambda e, t=t: e.matmul(phid[:, t * 128:(t + 1) * 128], hidT[:, t * 128:(t + 1) * 128], w2t[:], start=True, stop=True),
                                      [T_w2t, T_hidT], [T_phid])
                            S.add("act", lambda e: e.activation(out=ut[:, :], in_=phid[:, 0:256], func=AF.Copy), [T_phid], [T_ut])
                            S.add("dve", lambda e: e.tensor_copy(vcmp_aug[:, :, 64:128], ut[:].rearrange("p (t c) -> p t c", t=2)[:, :, 64:128]),
                                  [T_ut], [T_vcmp])
                    S.barrier()

                with contextlib.ExitStack() as pl:
                    Ws = [sb(pl, "Ws%d" % r, [128, 2560], BF16) for r in range(4)]
                    Ww = [sb(pl, "Ww%d" % r, [128, 1408], BF16) for r in range(4)]
                    T_Ws = [S.tt("Ws%d_%d" % (g, r)) for r in range(4)]
                    T_Ww = [S.tt("Ww%d_%d" % (g, r)) for r in range(4)]
                    ohc = Rot([(sb(pl, "ohc%d" % i, [33, 512], F32), S.tt("ohc%d_%d" % (g, i))) for i in range(2)])
                    tabrep = Rot([(sb(pl, "tabrB%d" % i, [33, 128], F32), S.tt("tabrB%d_%d" % (g, i))) for i in range(2)])
                    repc = Rot([(sb(pl, "repc%d" % i, [128, 512], F32), S.tt("repc%d_%d" % (g, i))) for i in range(2)])
                    stgs = Rot([(sb(pl, "stgS%d" % i, [128, 512], F32), S.tt("stgS%d_%d" % (g, i))) for i in range(2)])
                    pp2 = Rot([(ps(pl, "ppL%d" % i, [128, 512], F32), S.tt("ppL%d_%d" % (g, i))) for i in range(2)])
                    psc = Rot([(ps(pl, "pscL%d" % i, [128, 512], F32), S.tt("pscL%d_%d" % (g, i))) for i in range(2)])
                    ppv = Rot([(ps(pl, "ppvL%d" % i, [128, 512], F32), S.tt("ppvL%d_%d" % (g, i))) for i in range(2)])
                    pip = ps(pl, "pipL", [128, 512], F32); T_pip = S.tt("pipL%d" % g)
                    ptn = ps(pl, "ptnL", [128, 4, 128], BF16); T_ptn = S.tt("ptnL%d" % g)
                    T_repd = {}
                    for r in range(4):
                        hcol = 8 + 4 * g + r
                        slot = 4 * g + r
                        tr_, T_tr = tabrep.next()
                        S.add("dve", lambda e, tr_=tr_, hcol=hcol: e.tensor_scalar(tr_[:, :], zer33[:, :], tab[:, hcol:hcol + 1], None, ALU.add),
                              [T_tab, T_zer33], [T_tr])
                        for kind, L, X, rep_d, dst, T_dst in (("s", LS, 2560, repS_d, Ws[r], T_Ws[r]), ("w", LW, 1408, repW_d, Ww[r], T_Ww[r])):
                            T_rd = S.tt("repd_%s%d" % (kind, slot))
                            for i0 in range(0, L, 512):
                                n_ = min(512, L - i0)
                                oc, T_oc = ohc.next()
                                S.dma("sp", oc[:, 0:n_], c_oh_d[:, OH_OFF[kind] + i0:OH_OFF[kind] + i0 + n_], [T_in], [T_oc])
                                pb_, T_pb = pp2.next()
                                S.add("pe", lambda e, pb_=pb_, tr_=tr_, oc=oc, n_=n_: e.matmul(pb_[:, 0:n_], tr_[:, :], oc[:, 0:n_], start=True, stop=True),
                                      [T_tr, T_oc], [T_pb])
                                rc, T_rc = repc.next()
                                S.add("act", lambda e, pb_=pb_, rc=rc, n_=n_: e.activation(out=rc[:, 0:n_], in_=pb_[:, 0:n_], func=AF.Copy), [T_pb], [T_rc])
                                S.dma("pool", rep_d[slot, :, i0:i0 + n_], rc[:, 0:n_], [T_rc], [T_rd])
                            for x0 in range(0, X, 512):
                                n_ = min(512, X - x0)
                                sg_, T_sg = stgs.next()
                                src = bass.AP(rep_d.tensor, slot * 128 * L + 127 + x0, [[L - 1, 128], [1, n_]])
                                S.dma("sp", sg_[:, 0:n_], src, [T_rd], [T_sg])
                                S.add("act", lambda e, sg_=sg_, dst=dst, x0=x0, n_=n_: e.activation(out=dst[:, x0:x0 + n_], in_=sg_[:, 0:n_], func=AF.Exp),
                                      [T_sg], [T_dst])

                    exs = Rot([(sb(pl, "exL%d" % i, [128, 512], BF16), S.tt("exL%d_%d" % (g, i))) for i in range(3)])
                    pts = Rot([(sb(pl, "ptL%d" % i, [128, 512], BF16), S.tt("ptL%d_%d" % (g, i))) for i in range(4)])
                    grep_ = Rot([(sb(pl, "grep%d" % i, [128, 3, 512], F32), S.tt("grep%d_%d" % (g, i))) for i in range(4)])
                    rdt = Rot([(sb(pl, "rdL%d" % i, [128, 512], F32), S.tt("rdL%d_%d" % (g, i))) for i in range(2)])
                    fts = Rot([(sb(pl, "ftL%d" % i, [128, 512], F32), S.tt("ftL%d_%d" % (g, i))) for i in range(2)])
                    tms = Rot([(sb(pl, "tmL%d" % i, [128, 512], F32), S.tt("tmL%d_%d" % (g, i))) for i in range(2)])
                    acc = [sb(pl, "accL%d" % r, [128, 512], F32) for r in range(4)]
                    T_acc = [S.tt("accL%d_%d" % (g, r)) for r in range(4)]
                    impacc = sb(pl, "impacc", [128, 512], F32); T_impacc = S.tt("impacc%d" % g)
                    imod = sb(pl, "imod", [128, 4, 64], F32); T_imod = S.tt("imod%d" % g)
                    wrk = sb(pl, "wrk", [128, 64], F32); T_wrk = S.tt("wrk%d" % g)
                    m8a = sb(pl, "m8a", [128, 8], F32); T_m8a = S.tt("m8a%d" % g)
                    m8b = sb(pl, "m8b", [128, 8], F32); T_m8b = S.tt("m8b%d" % g)
                    negm = sb(pl, "negm", [128, 4, 128], BF16); T_negm = S.tt("negm%d" % g)
                    ntmp = sb(pl, "ntmp", [128, 512], BF16); T_ntmp = S.tt("ntmp%d" % g)
                    stage = Rot([(sb(pl, "stgO%d" % i, [128, 512], BF16), S.tt("stgO%d_%d" % (g, i))) for i in range(2)])
                    ptr32 = pip
                    S.add("dve", lambda e: e.memset(impacc[:], 0.0), [], [T_impacc])
                    S.add("dve", lambda e: e.memset(negm[:], 0.0), [], [T_negm])

                    def attend(r, c, kts, kT, T_kT, vaug, T_va, strip, T_strip, x0_of, first_mask=None):
                        pv, T_pv = ppv.next()
                        for ii, kt in enumerate(kts):
                            sc, T_sc = psc.next()
                            S.add("pe", lambda e, sc=sc, kt=kt: e.matmul(sc[:], kT[:, kt * 128:(kt + 1) * 128], QM[r][:, c * 512:(c + 1) * 512],
                                                                         start=True, stop=True), [T_kT, T_QMq[r], T_QMn[r][c]], [T_sc])
                            ex, T_ex = exs.next()
                            S.add("act", lambda e, sc=sc, ex=ex: e.activation(out=ex[:], in_=sc[:], func=AF.Exp), [T_sc], [T_ex])
                            x0 = x0_of(kt)
                            if x0 is None:
                                P_, T_P = ex, T_ex
                            else:
                                P_, T_P = pts.next()
                                S.add("dve", lambda e, ex=ex, P_=P_, x0=x0: e.tensor_tensor(P_[:], ex[:], strip[:, x0:x0 + 512], ALU.mult),
                                      [T_ex, T_strip], [T_P])
                            S.add("pe", lambda e, pv=pv, kt=kt, P_=P_, ii=ii: e.matmul(pv[:], vaug[:, kt, :], P_[:], start=(ii == 0), stop=(ii == len(kts) - 1)),
                                  [T_va, T_P], [T_pv])
                            if first_mask is not None:
                                first_mask(kt, P_, T_P, ii, len(kts))
                        return pv, T_pv

                    for c in range(8):
                        q0 = c * 512
                        cs = slice(q0, q0 + 512)
                        for r in range(4):
                            gr, T_gr = grep_.next()
                            for br in range(3):
                                pg, T_pg = pp2.next()
                                S.add("pe", lambda e, pg=pg, r=r, br=br: e.matmul(pg[:], selc[:, r * 3 + br, :], gT[:, cs], start=True, stop=True),
                                      [T_selc, T_gT], [T_pg])
                                S.add("act", lambda e, pg=pg, gr=gr, br=br: e.activation(out=gr[0:64, br, :], in_=pg[0:64, :], func=AF.Copy), [T_pg], [T_gr])
                            nt = 1 if c < 4 else 2

                            def impmm(kt, P_, T_P, ii, n_, r=r):
                                S.add("pe", lambda e: e.matmul(pip[:], mimp[:, kt, :], P_[:], start=(ii == 0), stop=(ii == n_ - 1)),
                                      [T_mimp, T_P], [T_pip])
                            pv, T_pv = attend(r, c, list(range(nt)), kcmpz, T_kcmpz, vcmp_aug, T_vcmp, MC, T_MC,
                                              lambda kt: (q0 - 2048 * kt) if (q0 - 2048 * kt) < 2560 else None, first_mask=impmm)
                            rd, T_rd = rdt.next()
                            S.add("dve", lambda e, pv=pv, rd=rd: e.tensor_scalar(rd[0:64, :], pv[0:64, :], 1e-30, None, ALU.max), [T_pv], [T_rd])
                            S.add("dve", lambda e, rd=rd: e.reciprocal(out=rd[0:64, :], in_=rd[0:64, :]), [T_rd], [T_rd])
                            if r == 0:
                                S.add("dve", lambda e, rd=rd: e.tensor_tensor(impacc[0:64, :], pip[0:64, :], rd[0:64, :], ALU.mult), [T_pip, T_rd], [T_impacc])
                            else:
                                tm, T_tm = tms.next()
                                S.add("dve", lambda e, rd=rd, tm=tm: e.tensor_tensor(tm[0:64, :], pip[0:64, :], rd[0:64, :], ALU.mult), [T_pip, T_rd], [T_tm])
                                S.add("dve", lambda e, tm=tm: e.tensor_tensor(impacc[0:64, :], impacc[0:64, :], tm[0:64, :], ALU.add), [T_impacc, T_tm], [T_impacc])
                            ft, T_ft = fts.next()
                            S.add("dve", lambda e, gr=gr, rd=rd, ft=ft: e.tensor_tensor(ft[0:64, :], gr[0:64, 0, :], rd[0:64, :], ALU.mult), [T_gr, T_rd], [T_ft])
                            S.add("dve", lambda e, pv=pv, ft=ft, r=r: e.tensor_tensor(acc[r][0:64, :], pv[64:128, :], ft[0:64, :], ALU.mult),
                                  [T_pv, T_ft], [T_acc[r]])
                            if r == 0:
                                greps = []
                            greps.append((gr, T_gr))
                        for qt in range(4):
                            S.add("pe", lambda e, qt=qt: e.transpose(ptr32[:, qt * 128:(qt + 1) * 128], impacc[:, qt * 128:(qt + 1) * 128], identf[:]),
                                  [T_impacc, T_identf], [T_pip])
                        for qt in range(4):
                            i_ = 4 * c + qt
                            y0 = 62 - 2 * i_
                            S.add("dve", lambda e, qt=qt, y0=y0: e.tensor_tensor(imod[:, qt, :], ptr32[:, qt * 128:qt * 128 + 64], keepS[:, y0:y0 + 64], ALU.mult),
                                  [T_pip, T_keepS], [T_imod])
                            S.add("dve", lambda e, qt=qt, y0=y0: e.tensor_tensor(imod[:, qt, :], imod[:, qt, :], addS[:, y0:y0 + 64], ALU.add),
                                  [T_imod, T_addS], [T_imod])
                        S.add("dve", lambda e: e.memset(imod[:, :, 0:1], 1e30), [], [T_imod])
                        for qt in range(4):
                            S.add("dve", lambda e, qt=qt: e.max(out=m8a[:], in_=imod[:, qt, :]), [T_imod], [T_m8a])
                            S.add("dve", lambda e, qt=qt: e.match_replace(out=wrk[:], in_to_replace=m8a[:], in_values=imod[:, qt, :], imm_value=-2e30),
                                  [T_imod, T_m8a], [T_wrk])
                            S.add("dve", lambda e: e.max(out=m8b[:], in_=wrk[:]), [T_wrk], [T_m8b])
                            S.add("dve", lambda e, qt=qt: e.tensor_scalar(negm[:, qt, 64:128], imod[:, qt, :], m8b[:, 7:8], NEG, ALU.is_lt, ALU.mult),
                                  [T_imod, T_m8b], [T_negm])
                        for qt in range(4):
                            S.add("pe", lambda e, qt=qt: e.transpose(ptn[:, qt, :], negm[:, qt, :], identb[:]), [T_negm, T_identb], [T_ptn])
                        S.add("act", lambda e: e.activation(out=ntmp[:], in_=ptn[:].rearrange("p a b -> p (a b)"), func=AF.Copy), [T_ptn], [T_ntmp])
                        for r in range(4):
                            S.add("dve", lambda e, r=r: e.tensor_copy(QM[r][64:128, cs], ntmp[64:128, :]), [T_ntmp], [T_QMn[r][c]])
                        for r in range(4):
                            gr, T_gr = greps[r]
                            for bi, (kT, T_kT, vaug, T_va, strip, T_strip, kts, x0f) in enumerate((
                                    (kslE, T_kslE, vsl_aug, T_vsl, Ws[r], T_Ws[r], list(range(0, 4 * c + 4)),
                                     lambda kt: min(q0 - kt * 128 + 384, 2048)),
                                    (kwz, T_kwz, vw_aug, T_vw, Ww[r], T_Ww[r], list(range(max(0, 4 * c - 4), 4 * c + 4)),
                                     lambda kt: q0 - kt * 128 + 384))):
                                pv, T_pv = attend(r, c, kts, kT, T_kT, vaug, T_va, strip, T_strip, x0f)
                                rd, T_rd = rdt.next()
                                S.add("dve", lambda e, pv=pv, rd=rd: e.reciprocal(out=rd[0:64, :], in_=pv[0:64, :]), [T_pv], [T_rd])
                                ft, T_ft = fts.next()
                                S.add("dve", lambda e, gr=gr, rd=rd, ft=ft, bi=bi: e.tensor_tensor(ft[0:64, :], gr[0:64, 1 + bi, :], rd[0:64, :], ALU.mult),
                                      [T_gr, T_rd], [T_ft])
                                tm, T_tm = tms.next()
                                S.add("dve", lambda e, pv=pv, ft=ft, tm=tm: e.tensor_tensor(tm[0:64, :], pv[64:128, :], ft[0:64, :], ALU.mult),
                                      [T_pv, T_ft], [T_tm])
                                S.add("dve", lambda e, tm=tm, r=r: e.tensor_tensor(acc[r][0:64, :], acc[r][0:64, :], tm[0:64, :], ALU.add),
                                      [T_acc[r], T_tm], [T_acc[r]])
                            if r % 2 == 0:
                                so, T_so = stage.next()
                            if r % 2 == 0:
                                S.add("act", lambda e, so=so, r=r: e.activation(out=so[0:64, :], in_=acc[r][0:64, :], func=AF.Copy), [T_acc[r]], [T_so])
                            else:
                                S.add("act", lambda e, so=so, r=r: e.activation(out=so[64:128, :], in_=acc[r][0:64, :], func=AF.Copy), [T_acc[r]], [T_so])
                                row0 = 512 + g * 256 + (r // 2) * 128
                                S.dma("pool", mixT_d[row0:row0 + 128, cs], so[:], [T_so], [T_mixT_d[4 + 2 * g + r // 2]])
                    S.barrier()

        if "A" in parts:
          with contextlib.ExitStack() as pa:
            EA = sb(pa, "EA", [128, 24, 256], BF16); T_EA = S.tt("EA")
            with contextlib.ExitStack() as pas:
                oha = sb(pas, "oha", [33, 3 * LA], F32); T_oha = S.tt("oha")
                S.dma("sp", oha[:], c_oh_d[:, 0:3 * LA], [T_in], [T_oha])
                tabrep = Rot([(sb(pas, "tabrep%d" % i, [33, 128], F32), S.tt("tabrep%d" % i)) for i in range(2)])
                repsb = Rot([(sb(pas, "repsb%d" % i, [128, 3 * LA], F32), S.tt("repsb%d" % i)) for i in range(2)])
                stg = Rot([(sb(pas, "stgA%d" % i, [128, 3, 256], F32), S.tt("stgA%d" % i)) for i in range(2)])
                pbs = Rot([(ps(pas, "pbs%d" % i, [128, 512], F32), S.tt("pbs%d" % i)) for i in range(2)])
                T_repA_d = [S.tt("repA_d%d" % i) for i in range(8)]
                for hh in range(8):
                    tr_, T_tr = tabrep.next()
                    S.add("dve", lambda e, tr_=tr_, hh=hh: e.tensor_scalar(tr_[:, :], zer33[:, :], tab[:, hh:hh + 1], None, ALU.add),
                          [T_tab, T_zer33], [T_tr])
                    rp, T_rp = repsb.next()
                    for p in range(3):
                        pb, T_pb = pbs.next()
                        S.add("pe", lambda e, pb=pb, tr_=tr_, p=p: e.matmul(pb[:, 0:LA], tr_[:, :], oha[:, p * LA:(p + 1) * LA],
                                                                        start=True, stop=True), [T_tr, T_oha], [T_pb])
                        S.add("act", lambda e, pb=pb, rp=rp, p=p: e.activation(out=rp[:, p * LA:(p + 1) * LA], in_=pb[:, 0:LA], func=AF.Copy),
                              [T_pb], [T_rp])
                    S.dma("pool", repA_d[hh * 3:(hh + 1) * 3].rearrange("v k l -> k v l"), rp[:].rearrange("k (v l) -> k v l", v=3),
                          [T_rp], [T_repA_d[hh]])
                    sg_, T_sg = stg.next()
                    for p in range(3):
                        src = bass.AP(repA_d.tensor, (hh * 3 + p) * 128 * LA + 127, [[LA - 1, 128], [1, 256]])
                        S.dma("sp", sg_[:, p, :], src, [T_repA_d[hh]], [T_sg])
                    S.add("act", lambda e, sg_=sg_, hh=hh: e.activation(out=EA[:, hh * 3:(hh + 1) * 3, :], in_=sg_[:], func=AF.Exp),
                          [T_sg], [T_EA])
                S.barrier()
            if "EA" in dbg_d:
                S.dma("pool", dbg_d["EA"].rearrange("v k x -> k v x"), EA[:], [T_EA], [T_y])

            wst = Rot([(sb(pa, "wst%d" % i, [128, 8, 128], F32), S.tt("wst%d" % i)) for i in range(2)])
            wq = sb(pa, "wq", [128, 8, 128], BF16); T_wq = S.tt("wq")
            wk = sb(pa, "wk", [128, 8, 128], BF16); T_wk = S.tt("wk")
            wv = sb(pa, "wv", [128, 8, 128], BF16); T_wv = S.tt("wv")
            aqT = sb(pa, "aqT", [128, SEQ], BF16); T_aqT = S.tt("aqT")
            akz = [sb(pa, "akz%d" % l, [128, SEQ], BF16) for l in range(2)]
            T_akz = [S.tt("akz%d" % l) for l in range(2)]
            avTp = sb(pa, "avTp", [128, SEQ], BF16); T_avTp = S.tt("avTp")
            vblk = [sb(pa, "vblk%d" % l, [128, 32, 128], BF16) for l in range(2)]
            T_vblk = [S.tt("vblk%d" % l) for l in range(2)]
            numacc = [sb(pa, "numacc%d" % l, [128, SEQ], F32) for l in range(2)]
            T_numacc = [S.tt("numacc%d" % l) for l in range(2)]
            rdt = sb(pa, "rdt", [128, 512], F32); T_rdt = S.tt("rdt")
            mst = Rot([(sb(pa, "mst%d" % i, [128, 1024], BF16), S.tt("mst%d" % i)) for i in range(2)])
            vtmp = Rot([(sb(pa, "vtmp%d" % i, [128, 8, 128], BF16), S.tt("vtmp%d" % i)) for i in range(2)])
            exs = Rot([(sb(pa, "exA%d" % i, [128, 2, 256], BF16), S.tt("exA%d" % i)) for i in range(3)])
            pts = Rot([(sb(pa, "ptA%d" % i, [128, 2, 256], BF16), S.tt("ptA%d" % i)) for i in range(4)])
            ppj = Rot([(ps(pa, "ppj%d" % i, [128, 512], F32), S.tt("ppj%d" % i)) for i in range(2)])
            ptr = Rot([(ps(pa, "ptrA%d" % i, [128, 8, 128], BF16), S.tt("ptrA%d" % i)) for i in range(1)])
            psc = Rot([(ps(pa, "pscA%d" % i, [128, 2, 256], F32), S.tt("pscA%d" % i)) for i in range(3)])
            ppv = Rot([(ps(pa, "ppvA%d" % i, [128, 512], F32), S.tt("ppvA%d" % i)) for i in range(2)])
            S.add("dve", lambda e: e.memset(vblk[0][:, :, 64:128], 1.0), [], [T_vblk[0]])
            S.add("dve", lambda e: e.memset(vblk[1][:, :, 0:64], 1.0), [], [T_vblk[1]])
            S.add("dve", lambda e: e.memset(akz[0][64:128, :], 0.0), [], [T_akz[0]])
            S.add("dve", lambda e: e.memset(akz[1][0:64, :], 0.0), [], [T_akz[1]])

            for hp in range(4 if A_STOP is None else 1):
                load_w_in(wst, hp * 128, 128, wq, T_wq)
                load_w_in(wst, 512 + hp * 128, 128, wk, T_wk)
                load_w_in(wst, 1024 + hp * 128, 128, wv, T_wv)
                for c in range(8):
                    pp, T_pp = ppj.next()
                    proj512(pp, T_pp, wq, T_wq, c)
                    S.add("act", lambda e, pp=pp, c=c: e.activation(out=aqT[:, c * 512:(c + 1) * 512], in_=pp[:], func=AF.Copy, scale=0.125),
                          [T_pp], [T_aqT])
                    pp, T_pp = ppj.next()
                    proj512(pp, T_pp, wk, T_wk, c)
                    S.add("act", lambda e, pp=pp, c=c: e.activation(out=akz[0][0:64, c * 512:(c + 1) * 512], in_=pp[0:64, :], func=AF.Copy),
                          [T_pp], [T_akz[0]])
                    S.add("dve", lambda e, pp=pp, c=c: e.tensor_copy(akz[1][64:128, c * 512:(c + 1) * 512], pp[64:128, :]),
                          [T_pp], [T_akz[1]])
                for p, (win, r) in enumerate(DIL):
                    nb = 32 // r
                    gs = min(4, nb)
                    clen = SEQ // r
                    csz = min(512, clen)
                    for u0 in range(0, SEQ, csz):
                        cres, i0 = u0 // clen, u0 % clen
                        pp, T_pp = ppj.next()
                        for j in range(8):
                            S.add("pe", lambda e, pp=pp, j=j, cres=cres, i0=i0, r=r, csz=csz: e.matmul(
                                pp[:, 0:csz], wv[:, j, :], xnT[:, j, sl(i0 * r + cres, csz, r)], start=(j == 0), stop=(j == 7)),
                                [T_wv] + T_xnT, [T_pp])
                        S.add("act", lambda e, pp=pp, u0=u0, csz=csz: e.activation(out=avTp[:, u0:u0 + csz], in_=pp[:, 0:csz], func=AF.Copy),
                              [T_pp], [T_avTp])
                    for m0 in range(0, 32, 8):
                        pt, T_pt = ptr.next()
                        for mm in range(8):
                            m = m0 + mm
                            S.add("pe", lambda e, pt=pt, mm=mm, m=m: e.transpose(pt[:, mm, :], avTp[:, m * 128:(m + 1) * 128], identb[:]),
                                  [T_avTp, T_identb], [T_pt])
                        vt, T_vt = vtmp.next()
                        S.add("act", lambda e, pt=pt, vt=vt: e.activation(out=vt[:], in_=pt[:], func=AF.Copy), [T_pt], [T_vt])
                        S.add("dve", lambda e, vt=vt, m0=m0: e.tensor_copy(vblk[0][:, m0:m0 + 8, 0:64], vt[:, :, 0:64]), [T_vt], [T_vblk[0]])
                        S.add("dve", lambda e, vt=vt, m0=m0: e.tensor_copy(vblk[1][:, m0:m0 + 8, 64:128], vt[:, :, 64:128]), [T_vt], [T_vblk[1]])
                    if A_STOP == "vblk":
                        continue
                    for l in range(2):
                        hh = 2 * hp + l
                        prevP = None
                        pv = T_pv = None
                        for m in range(32):
                            cres, n = m // nb, m % nb
                            half = m % 2
                            if half == 0:
                                sc, T_sc = psc.next()
                            st = 128 * n * r + cres
                            nq = 256 if n < nb - 1 else 128
                            S.add("pe", lambda e, sc=sc, half=half, st=st, r=r, nq=nq, l=l: e.matmul(
                                sc[:, half, 0:nq], akz[l][:, sl(st, 128, r)], aqT[:, sl(st, nq, r)],
                                start=True, stop=True), [T_akz[l], T_aqT], [T_sc])
                            if half == 1:
                                ex, T_ex = exs.next()
                                S.add("act", lambda e, sc=sc, ex=ex: e.activation(out=ex[:], in_=sc[:], func=AF.Exp), [T_sc], [T_ex])
                                pt_, T_pt_ = pts.next()
                                for hf in range(2):
                                    S.add("dve", lambda e, ex=ex, pt_=pt_, hf=hf, idx=hh * 3 + p: e.tensor_tensor(
                                        pt_[:, hf, :], ex[:, hf, :], EA[:, idx, :], ALU.mult), [T_ex, T_EA], [T_pt_])
                                for mm in (m - 1, m):
                                    c2, n2 = mm // nb, mm % nb
                                    if n2 % gs == 0:
                                        pv, T_pv = ppv.next()
                                    qcol = (n2 % gs) * 128
                                    curP = (pt_, T_pt_, mm % 2)
                                    if n2 > 0:
                                        S.add("pe", lambda e, pv=pv, qcol=qcol, mm=mm, l=l, pp_=prevP: e.matmul(
                                            pv[:, qcol:qcol + 128], vblk[l][:, mm - 1, :], pp_[0][:, pp_[2], 128:256],
                                            start=True, stop=False), [T_vblk[l], prevP[1]], [T_pv])
                                    S.add("pe", lambda e, pv=pv, qcol=qcol, mm=mm, l=l, cp=curP, n2=n2: e.matmul(
                                        pv[:, qcol:qcol + 128], vblk[l][:, mm, :], cp[0][:, cp[2], 0:128],
                                        start=(n2 == 0), stop=True), [T_vblk[l], curP[1]], [T_pv])
                                    prevP = curP
                                    if n2 % gs == gs - 1:
                                        nbase = n2 - (gs - 1)
                                        st0 = 128 * nbase * r + c2
                                        dst = numacc[l][:, sl(st0, gs * 128, r)]
                                        if p == 0:
                                            S.add("act", lambda e, pv=pv, dst=dst, gs=gs: e.activation(out=dst, in_=pv[:, 0:gs * 128], func=AF.Copy),
                                                  [T_pv], [T_numacc[l]])
                                        else:
                                            S.add("dve", lambda e, pv=pv, dst=dst, gs=gs: e.tensor_tensor(dst, pv[:, 0:gs * 128], dst, ALU.add),
                                                  [T_pv, T_numacc[l]], [T_numacc[l]])
                if A_STOP == "vblk":
                    continue
                for cc in range(4):
                    ms, T_ms = mst.next()
                    for l in range(2):
                        nrow = slice(0, 64) if l == 0 else slice(64, 128)
                        drow = slice(64, 128) if l == 0 else slice(0, 64)
                        for c5 in range(2):
                            cs = slice(cc * 1024 + c5 * 512, cc * 1024 + (c5 + 1) * 512)
                            S.add("dve", lambda e, cs=cs, nrow=nrow, drow=drow, l=l: e.reciprocal(out=rdt[nrow, :], in_=numacc[l][drow, cs]),
                                  [T_numacc[l]], [T_rdt])
                            S.add("dve", lambda e, cs=cs, nrow=nrow, l=l, ms=ms, c5=c5: e.tensor_tensor(
                                ms[nrow, c5 * 512:(c5 + 1) * 512], numacc[l][nrow, cs], rdt[nrow, :], ALU.mult),
                                [T_numacc[l], T_rdt], [T_ms])
                    S.dma("pool", mixT_d[hp * 128:(hp + 1) * 128, cc * 1024:(cc + 1) * 1024], ms[:], [T_ms], [T_mixT_d[hp]])
            S.barrier()
        else:
            for j in range(4):
                S.dma("pool", mixT_d[j * 128:(j + 1) * 128, :], xnT[:, j, :], T_xnT, [T_mixT_d[j]])
        if "B" not in parts:
            for j in range(4, 8):
                S.dma("pool", mixT_d[j * 128:(j + 1) * 128, :], xnT[:, j, :], T_xnT, [T_mixT_d[j]])
            S.barrier()
            ph1.close()
        else:
            S.barrier()
            ph1.close()
            build_B()
        if "mixT" in dbg_d:
            for j in range(8):
                S.dma("pool", dbg_d["mixT"][j * 128:(j + 1) * 128, :], mixT_d[j * 128:(j + 1) * 128, :], [T_mixT_d[j]], [T_y])
            S.barrier()

        with contextlib.ExitStack() as p2:
            TCH = 256
            NCH = SEQ // TCH
            wo = sb(p2, "wo", [128, 8, D], BF16); T_wo = S.tt("wo")
            wg = sb(p2, "wg", [128, 8, FFN], BF16); T_wg = S.tt("wg")
            wu = sb(p2, "wu", [128, 8, FFN], BF16); T_wu = S.tt("wu")
            wd = sb(p2, "wd", [128, NHC, D], BF16); T_wd = S.tt("wd")
            gf = sb(p2, "gf", [128, D], F32); T_gf = S.tt("gf")
            S.dma("sp", gf[:], bass.AP(gf_d.tensor, 0, [[0, 128], [1, D]]), [T_in], [T_gf])
            f32b = Rot([(sb(p2, "f32b%d" % i, [128, D], F32), S.tt("f32b%d" % i)) for i in range(3)])
            cast_eng = Rot(["dve", "act"])

            def cast(eng, out, in_, scal, reads, writes):
                if eng == "act":
                    if scal is None:
                        S.add("act", lambda e: e.activation(out=out, in_=in_, func=AF.Copy), reads, writes)
                    else:
                        S.add("act", lambda e: e.activation(out=out, in_=in_, func=AF.Copy, scale=scal), reads, writes)
                else:
                    if scal is None:
                        S.add(eng, lambda e: e.tensor_copy(out, in_), reads, writes)
                    else:
                        S.add(eng, lambda e: e.tensor_scalar(out, in_, scal, None, ALU.mult), reads, writes)

            def load_w(dst, T_dst, src_d, nrow_chunks, ncols, gT, T_gT):
                for j in range(nrow_chunks):
                    for c0 in range(0, ncols, D):
                        cw = min(D, ncols - c0)
                        stg, T_stg = f32b.next()
                        S.dma("sp", stg[:, 0:cw], src_d[j * 128:(j + 1) * 128, c0:c0 + cw], [T_in], [T_stg])
                        eng = cast_eng.next()
                        if gT is None:
                            cast(eng, dst[:, j, c0:c0 + cw], stg[:, 0:cw], None, [T_stg], [T_dst])
                        else:
                            cast(eng, dst[:, j, c0:c0 + cw], stg[:, 0:cw], gT[:, j:j + 1], [T_stg, T_gT], [T_dst])

            load_w(wo, T_wo, w_out_d, 8, D, None, None)
            load_w(wg, T_wg, w_gate_d, 8, FFN, g2T, T_g2T)
            load_w(wu, T_wu, w_up_d, 8, FFN, g2T, T_g2T)
            load_w(wd, T_wd, w_down_d, NHC, D, None, None)

            mixc = Rot([(sb(p2, "mixc%d" % i, [128, 8, TCH], BF16), S.tt("mixc%d" % i)) for i in range(2)])
            h1 = sb(p2, "h1", [128, TCH // 128, D], F32); T_h1 = [S.tt("h1_%d" % i) for i in range(TCH // 128)]
            hnb = Rot([(sb(p2, "hnb%d" % i, [128, D], BF16), S.tt("hnb%d" % i)) for i in range(2)])
            hnT = sb(p2, "hnT", [128, 8, TCH], BF16); T_hnT = [S.tt("hnT%d" % i) for i in range(TCH // 128)]
            actT = sb(p2, "actT", [128, NHC, TCH], BF16); T_actT = [S.tt("actT%d" % i) for i in range(NHC)]
            sg = Rot([(sb(p2, "sg%d" % i, [128, TCH], F32), S.tt("sg%d" % i)) for i in range(2)])
            junk2 = sb(p2, "junk2", [128, D], BF16); T_junk2 = S.tt("junk2")
            st2 = sb(p2, "st2", [128, 8], F32); T_st2 = S.tt("st2")
            pyo = Rot([(ps(p2, "pyo%d" % i, [128, D], F32), S.tt("pyo%d" % i)) for i in range(2)])
            pgu = Rot([(ps(p2, "pgu%d" % i, [128, 2, TCH], F32), S.tt("pgu%d" % i)) for i in range(3)])
            ptr2 = ps(p2, "ptr2", [128, 8, 128], BF16); T_ptr2 = S.tt("ptr2")

            def rms_stats(src, T_src, col):
                S.add("act", lambda e: e.activation(out=junk2[:], in_=src, func=AF.Square, accum_out=st2[:, col:col + 1]),
                      [T_src], [T_junk2, T_st2])
                S.add("act", lambda e: e.activation(out=st2[:, col + 1:col + 2], in_=st2[:, col:col + 1], func=AF.Sqrt,
                                                    scale=1.0 / D, bias=EPS), [T_st2], [T_st2])
                S.add("dve", lambda e: e.reciprocal(out=st2[:, col + 2:col + 3], in_=st2[:, col + 1:col + 2]), [T_st2], [T_st2])
                return st2[:, col + 2:col + 3]

            for ch in range(NCH):
                t0 = ch * TCH
                mc, T_mc = mixc.next()
                S.dma("sp", mc[:], mixT_d.rearrange("(j p) t -> p j t", p=128)[:, :, t0:t0 + TCH], T_mixT_d, [T_mc])
                for tt in range(TCH // 128):
                    tok0 = t0 + tt * 128
                    xt, T_xt = f32b.next()
                    S.dma("sp", xt[:], x_d[tok0:tok0 + 128, :], [T_in], [T_xt])
                    yo, T_yo = pyo.next()
                    for half in range(2):
                        for j in range(8):
                            S.add("pe", lambda e, yo=yo, mc=mc, j=j, tt=tt, half=half: e.matmul(
                                yo[:, half * 512:(half + 1) * 512], mc[:, j, tt * 128:(tt + 1) * 128],
                                wo[:, j, half * 512:(half + 1) * 512], start=(j == 0), stop=(j == 7)),
                                [T_mc, T_wo], [T_yo])
                    S.add("dve", lambda e, yo=yo, xt=xt, tt=tt: e.tensor_tensor(h1[:, tt, :], yo[:], xt[:], ALU.add),
                          [T_yo, T_xt], [T_h1[tt]])
                    rstd = rms_stats(h1[:, tt, :], T_h1[tt], 0)
                    hb, T_hb = hnb.next()
                    S.add("dve", lambda e, hb=hb, tt=tt, rstd=rstd: e.tensor_scalar(hb[:], h1[:, tt, :], rstd, None, ALU.mult),
                          [T_h1[tt], T_st2], [T_hb])
                    for j in range(8):
                        S.add("pe", lambda e, hb=hb, j=j: e.transpose(ptr2[:, j, :], hb[:, j * 128:(j + 1) * 128], identb[:]),
                              [T_hb, T_identb], [T_ptr2])
                    S.add("act", lambda e, tt=tt: e.activation(out=hnT[:, :, tt * 128:(tt + 1) * 128], in_=ptr2[:], func=AF.Copy),
                          [T_ptr2], [T_hnT[tt]])
                for hc in range(NHC):
                    pg, T_pg = pgu.next()
                    for which, wsrc, T_w in ((0, wg, T_wg), (1, wu, T_wu)):
                        for j in range(8):
                            S.add("pe", lambda e, pg=pg, which=which, wsrc=wsrc, j=j, hc=hc: e.matmul(
                                pg[:, which, :], wsrc[:, j, hc * 128:(hc + 1) * 128], hnT[:, j, :],
                                start=(j == 0), stop=(j == 7)), [T_w] + T_hnT, [T_pg])
                    s_, T_s = sg.next()
                    S.add("act", lambda e, pg=pg, s_=s_: e.activation(out=s_[:], in_=pg[:, 0, :], func=AF.Silu), [T_pg], [T_s])
                    S.add("dve", lambda e, pg=pg, s_=s_, hc=hc: e.tensor_tensor(actT[:, hc, :], pg[:, 1, :], s_[:], ALU.mult),
                          [T_pg, T_s], [T_actT[hc]])
                for tt in range(TCH // 128):
                    tok0 = t0 + tt * 128
                    dn, T_dn = pyo.next()
                    for half in range(2):
                        for hc in range(NHC):
                            S.add("pe", lambda e, dn=dn, hc=hc, tt=tt, half=half: e.matmul(
                                dn[:, half * 512:(half + 1) * 512], actT[:, hc, tt * 128:(tt + 1) * 128],
                                wd[:, hc, half * 512:(half + 1) * 512], start=(hc == 0), stop=(hc == NHC - 1)),
                                T_actT + [T_wd], [T_dn])
                    S.add("dve", lambda e, dn=dn, tt=tt: e.tensor_tensor(h1[:, tt, :], dn[:], h1[:, tt, :], ALU.add),
                          [T_dn, T_h1[tt]], [T_h1[tt]])
                    rstd = rms_stats(h1[:, tt, :], T_h1[tt], 4)
                    ot, T_ot = f32b.next()
                    S.add("dve", lambda e, ot=ot, tt=tt, rstd=rstd: e.scalar_tensor_tensor(
                        ot[:], h1[:, tt, :], rstd, gf[:], ALU.mult, ALU.mult), [T_h1[tt], T_st2, T_gf], [T_ot])
                    S.dma("pool", y_d[tok0:tok0 + 128, :], ot[:], [T_ot], [T_y])
            S.barrier()
        S.emit(outer, verbose=True)
    return nc


_CONSTS = None


def make_in_maps(inputs, ncores=8):
    global _CONSTS
    if _CONSTS is None:
        _CONSTS = build_consts()
    f = lambda a: np.ascontiguousarray(np.asarray(a, dtype=np.float32))
    shared = {
        "norm1_g": f(inputs["norm1_g"]).reshape(1, D),
        "w_in": f(inputs["w_in"]).reshape(D, IN_W),
        "rel_bias": f(inputs["rel_bias"]),
        "cmp_pos": f(inputs["cmp_pos"]).reshape(32, HD),
        "cmp_k_w1": f(inputs["cmp_k_w1"]).reshape(32, HD, 128),
        "cmp_k_b1": f(inputs["cmp_k_b1"]).reshape(1, 128),
        "cmp_k_w2": f(inputs["cmp_k_w2"]).reshape(128, HD),
        "cmp_v_w1": f(inputs["cmp_v_w1"]).reshape(32, HD, 128),
        "cmp_v_b1": f(inputs["cmp_v_b1"]).reshape(1, 128),
        "cmp_v_w2": f(inputs["cmp_v_w2"]).reshape(128, HD),
        "w_out": f(inputs["w_out"]).reshape(D, D),
        "norm2_g": f(inputs["norm2_g"]).reshape(1, D),
        "w_gate": f(inputs["w_gate"]).reshape(D, FFN),
        "w_up": f(inputs["w_up"]).reshape(D, FFN),
        "w_down": f(inputs["w_down"]).reshape(FFN, D),
        "norm_f_g": f(inputs["norm_f_g"]).reshape(1, D),
    }
    shared.update(_CONSTS)
    x = f(inputs["x"])
    maps = []
    for b in range(ncores):
        m = dict(shared)
        m["x"] = np.ascontiguousarray(x[b])
        maps.append(m)
    return maps


def kernel(**inputs):
    nc = build_program()
    in_maps = make_in_maps(inputs, 8)
    res = run_bass_kernel_spmd(nc, in_maps, core_ids=list(range(8)))
    return np.stack([np.asarray(r["y"], dtype=np.float32) for r in res.results], axis=0)
```

```python
import contextlib
import math
import types
import numpy as np
import ml_dtypes
import concourse.bass as bass
import concourse.mybir as mybir
from concourse.bass_utils import run_bass_kernel_spmd

F32 = mybir.dt.float32
BF16 = mybir.dt.bfloat16
ALU = mybir.AluOpType
AF = mybir.ActivationFunctionType

SEQ = 4096
D = 1024
NT = SEQ // 128
HD = 64
FFN = 2816
NHC = FFN // 128
IN_W = 2840
REL_BUCKETS = 32
NEG = -30000.0
EPS = 1e-6
DIL = ((128, 1), (512, 4), (2048, 16))

RAW, WAW, WAR = 1, 2, 4


class TT:
    __slots__ = ("name", "w", "r", "sem")

    def __init__(self, name):
        self.name = name
        self.w = None
        self.r = {}
        self.sem = None


class Op:
    __slots__ = ("eng", "fn", "deps", "dma", "semkey", "pos", "val", "signal", "waits")


class Sched:
    ENGS = ("pe", "act", "dve", "pool", "sp")

    def __init__(self, nc):
        self.nc = nc
        self.ops = []
        self.eobj = {"pe": nc.tensor, "act": nc.scalar, "dve": nc.vector, "pool": nc.gpsimd, "sp": nc.sync}
        self.ndsem = 0
        self.all_tt = []

    def tt(self, name):
        t = TT(name)
        self.all_tt.append(t)
        return t

    @staticmethod
    def _freeze(fn):
        if fn is None or fn.__closure__ is None:
            return fn
        cells = []
        for c in fn.__closure__:
            try:
                v = c.cell_contents
            except ValueError:
                cells.append(c)
                continue
            if isinstance(v, types.FunctionType) and v.__closure__ is not None and v is not fn:
                v = Sched._freeze(v)
            cells.append(types.CellType(v))
        return types.FunctionType(fn.__code__, fn.__globals__, fn.__name__, fn.__defaults__, tuple(cells))

    def add(self, eng, fn, reads=(), writes=(), dma=False):
        fn = self._freeze(fn)
        op = Op()
        idx = len(self.ops)
        op.eng, op.fn, op.dma = eng, fn, dma
        op.signal = False
        deps = {}
        for t in reads:
            if t.w is not None:
                deps[t.w] = deps.get(t.w, 0) | RAW
        for t in writes:
            if t.w is not None:
                deps[t.w] = deps.get(t.w, 0) | WAW
            for ri in t.r.values():
                deps[ri] = deps.get(ri, 0) | WAR
        deps.pop(idx, None)
        op.deps = deps
        if dma:
            t0 = writes[0]
            if t0.sem is None:
                t0.sem = self.ndsem
                self.ndsem += 1
            op.semkey = ("d", t0.sem)
        else:
            op.semkey = eng
        rkey = op.semkey
        for t in reads:
            t.r[rkey] = idx
        for t in writes:
            t.w = idx
            t.r = {}
        self.ops.append(op)
        return idx

    def dma(self, eng, out, in_, reads, writes, **kw):
        return self.add(eng, lambda e: e.dma_start(out=out, in_=in_, **kw), reads, writes, dma=True)

    def barrier(self):
        deps = {}
        for t in self.all_tt:
            if t.w is not None:
                deps[t.w] = RAW | WAW | WAR
            for ri in t.r.values():
                deps[ri] = RAW | WAW | WAR
        for k in self.ENGS:
            op = Op()
            op.eng, op.fn, op.dma, op.signal = k, None, False, False
            op.deps = dict(deps)
            op.semkey = k
            self.ops.append(op)

    def emit(self, stack, verbose=False):
        ops = self.ops
        pos = {k: 0 for k in self.ENGS}
        dcount = {}
        clock = {k: {} for k in self.ENGS}
        done_clock = [None] * len(ops)
        for i, op in enumerate(ops):
            E = op.eng
            ck = clock[E]
            waits = {}
            for d, kind in op.deps.items():
                dop = ops[d]
                if dop.fn is None:
                    continue
                key = dop.semkey
                if (not dop.dma) and dop.eng == E and (not op.dma) and op.fn is not None:
                    if E == "pe":
                        continue
                if ck.get(key, 0) >= dop.pos:
                    continue
                if waits.get(key, (0, 0))[0] < dop.pos:
                    waits[key] = (dop.pos, d)
            for key, (p, d) in waits.items():
                ops[d].signal = True
                for k2, v2 in done_clock[d].items():
                    if ck.get(k2, 0) < v2:
                        ck[k2] = v2
            op.waits = [(key, d) for key, (p, d) in waits.items()]
            if op.fn is None:
                op.pos = 0
                done_clock[i] = None
                continue
            if op.dma:
                sk = op.semkey
                dcount[sk] = dcount.get(sk, 0) + 1
                op.pos = dcount[sk]
            else:
                pos[E] += 1
                op.pos = pos[E]
            snap = dict(ck)
            snap[op.semkey] = op.pos
            done_clock[i] = snap
        cnt = {k: 0 for k in self.ENGS}
        for op in ops:
            if op.fn is None:
                continue
            if op.dma:
                op.val = 16 * op.pos
            else:
                if op.signal:
                    cnt[op.eng] += 1
                op.val = cnt[op.eng]
        nc = self.nc
        esem = {k: stack.enter_context(nc.semaphore("s_" + k)) for k in self.ENGS}
        dsem = [stack.enter_context(nc.semaphore("d%d" % j)) for j in range(self.ndsem)]
        nw = 0
        for op in ops:
            e = self.eobj[op.eng]
            for key, d in op.waits:
                sem = esem[key] if isinstance(key, str) else dsem[key[1]]
                e.wait_ge(sem, ops[d].val)
                nw += 1
            if op.fn is None:
                continue
            ins = op.fn(e)
            if op.dma:
                ins.then_inc(dsem[op.semkey[1]], 16)
            elif op.signal:
                ins.then_inc(esem[op.eng], 1)
        if verbose:
            print("[sched] ops", len(ops), "waits", nw, "signals", cnt, "dma sems", self.ndsem, flush=True)


def sl(start, count, step=1):
    return slice(start, start + (count - 1) * step + 1, step)


class Rot:
    def __init__(self, items):
        self.items = items
        self.i = 0

    def next(self):
        it = self.items[self.i % len(self.items)]
        self.i += 1
        return it


def _t5_bucket_np(dist):
    dist = np.asarray(dist, np.int64)
    max_exact = REL_BUCKETS // 2
    df = np.maximum(dist, 1).astype(np.float32)
    large = max_exact + (np.log(df / np.float32(max_exact)) / np.float32(math.log(2048 / max_exact))
                         * np.float32(REL_BUCKETS - max_exact)).astype(np.int32)
    large = np.minimum(large, REL_BUCKETS - 1)
    return np.where(dist < max_exact, dist, large)


LA = 256 + 127
LS = 2560 + 127
LW = 1408 + 127
OH_OFF = {"a0": 0, "a1": LA, "a2": 2 * LA, "s": 3 * LA, "w": 3 * LA + LS}
OH_LEN = 3 * LA + LS + LW


def _onehot_const():
    oh = np.zeros((33, OH_LEN), np.float32)

    def fill(off, L, dist_of_i, valid_of_i):
        i = np.arange(L)
        dist = dist_of_i(i)
        valid = valid_of_i(dist)
        b = _t5_bucket_np(np.maximum(dist, 0))
        for j in range(L):
            if valid[j]:
                oh[b[j], off + j] = 1.0
            else:
                oh[32, off + j] = 1.0

    for p, (win, dil) in enumerate(DIL):
        fill(OH_OFF["a%d" % p], LA, lambda i, dil=dil: (i - 127) * dil, lambda d, dil=dil: (d >= 0) & (d <= 128 * dil))
    fill(OH_OFF["s"], LS, lambda i: i - 127 - 384, lambda d: d >= 0)
    fill(OH_OFF["w"], LW, lambda i: i - 127 - 384, lambda d: (d >= 0) & (d < 512))
    return oh


def build_consts():
    c = {}
    c["c_oh"] = _onehot_const()
    c["c_ident_bf"] = np.eye(128, dtype=np.float32).astype(ml_dtypes.bfloat16)
    c["c_ident_f"] = np.eye(128, dtype=np.float32)
    bf = ml_dtypes.bfloat16
    key = np.arange(SEQ)
    c["c_ex"] = (key[None, :] // 64 == np.arange(64)[:, None]).astype(np.float32).astype(bf)
    n = np.arange(128)[:, None]
    xx = np.arange(2560)[None, :]
    c["c_mc"] = ((xx - 16 * n - 31) >= 0).astype(np.float32).astype(bf)
    mimp = np.zeros((256, 128), np.float32)
    for j in range(64):
        for off, w in zip(range(-1, 4), (1.0, 2.0, 2.0, 2.0, 1.0)):
            nn = 4 * j + off
            if 0 <= nn <= 254:
                mimp[nn, j] = w
    c["c_mimp"] = np.ascontiguousarray(mimp.reshape(2, 128, 128).transpose(1, 0, 2)).astype(bf)
    keep = np.zeros((128, 126), np.float32)
    add = np.zeros((128, 126), np.float32)
    for p in range(128):
        for y in range(126):
            delta = y - 62 - (1 if p >= 64 else 0)
            if delta > 0:
                add[p, y] = -1e30
            elif delta == 0:
                add[p, y] = 3e30
            elif delta == -1:
                add[p, y] = 2e30
            else:
                keep[p, y] = 1.0
    c["c_keep"] = keep
    c["c_add"] = add
    sel = np.zeros((128, 12, 128), np.float32)
    for r in range(12):
        sel[r, r, :] = 1.0
    c["c_sel"] = sel
    return c


A_STOP = None


def build_program(parts=("A", "B"), debug=()):
    nc = bass.Bass("TRN2", target_bir_lowering=False)
    S = Sched(nc)

    def din(name, shape, dt=F32):
        return nc.dram_tensor(name, list(shape), dt, kind="ExternalInput").ap()

    x_d = din("x", [SEQ, D])
    g1_d = din("norm1_g", [1, D])
    w_in_d = din("w_in", [D, IN_W])
    relb_d = din("rel_bias", [REL_BUCKETS, 16])
    pos_d = din("cmp_pos", [32, HD])
    kw1_d = din("cmp_k_w1", [32, HD, 128])
    kb1_d = din("cmp_k_b1", [1, 128])
    kw2_d = din("cmp_k_w2", [128, HD])
    vw1_d = din("cmp_v_w1", [32, HD, 128])
    vb1_d = din("cmp_v_b1", [1, 128])
    vw2_d = din("cmp_v_w2", [128, HD])
    w_out_d = din("w_out", [D, D])
    g2_d = din("norm2_g", [1, D])
    w_gate_d = din("w_gate", [D, FFN])
    w_up_d = din("w_up", [D, FFN])
    w_down_d = din("w_down", [FFN, D])
    gf_d = din("norm_f_g", [1, D])
    c_oh_d = din("c_oh", [33, OH_LEN])
    c_identb_d = din("c_ident_bf", [128, 128], BF16)
    c_identf_d = din("c_ident_f", [128, 128])
    c_ex_d = din("c_ex", [64, SEQ], BF16)
    c_mc_d = din("c_mc", [128, 2560], BF16)
    c_mimp_d = din("c_mimp", [128, 2, 128], BF16)
    c_keep_d = din("c_keep", [128, 126])
    c_add_d = din("c_add", [128, 126])
    c_sel_d = din("c_sel", [128, 12, 128])
    y_d = nc.dram_tensor("y", [SEQ, D], F32, kind="ExternalOutput").ap()
    dbg_d = {}
    for name, shape, dt in debug:
        dbg_d[name] = nc.dram_tensor("dbg_" + name, list(shape), dt, kind="ExternalOutput").ap()

    xnT_d = nc.dram_tensor("xnT_scr", [D, SEQ], BF16, kind="Internal").ap()
    mixT_d = nc.dram_tensor("mixT_scr", [D, SEQ], BF16, kind="Internal").ap()
    T_xnT_d = S.tt("xnT_d")
    T_mixT_d = [S.tt("mixT_d%d" % j) for j in range(8)]
    T_in = S.tt("inputs")
    T_y = S.tt("y")

    outer = contextlib.ExitStack()
    with outer:
        uniq = [0]

        def sb(stack, name, shape, dt):
            uniq[0] += 1
            return stack.enter_context(nc.sbuf_tensor("%s_%d" % (name, uniq[0]), list(shape), dt))

        def ps(stack, name, shape, dt):
            uniq[0] += 1
            return stack.enter_context(nc.psum_tensor("%s_%d" % (name, uniq[0]), list(shape), dt))

        identb = sb(outer, "identb", [128, 128], BF16); T_identb = S.tt("identb")
        identf = sb(outer, "identf", [128, 128], F32); T_identf = S.tt("identf")
        g1T = sb(outer, "g1T", [128, 8], F32); T_g1T = S.tt("g1T")
        g2T = sb(outer, "g2T", [128, 8], F32); T_g2T = S.tt("g2T")
        S.dma("sp", identb[:], c_identb_d[:, :], [T_in], [T_identb])
        S.dma("sp", identf[:], c_identf_d[:, :], [T_in], [T_identf])
        S.dma("sp", g1T[:], g1_d.rearrange("o (j p) -> p (o j)", p=128), [T_in], [T_g1T],
              allow_slow_non_contiguous=True)
        S.dma("sp", g2T[:], g2_d.rearrange("o (j p) -> p (o j)", p=128), [T_in], [T_g2T],
              allow_slow_non_contiguous=True)

        tab = sb(outer, "tab", [33, 16], F32); T_tab = S.tt("tab")
        S.dma("sp", tab[0:32, :], relb_d[:, :], [T_in], [T_tab])
        S.add("dve", lambda e: e.memset(tab[32:33, :], NEG), [], [T_tab])
        zer33 = sb(outer, "zer33", [33, 128], F32); T_zer33 = S.tt("zer33")
        S.add("dve", lambda e: e.memset(zer33[:, :], 0.0), [], [T_zer33])

        ph1 = contextlib.ExitStack()
        xnT = sb(ph1, "xnT", [128, 8, SEQ], BF16)
        T_xnT = [S.tt("xnT_t%d" % i) for i in range(NT)]
        with contextlib.ExitStack() as p0:
            xts = Rot([(sb(p0, "xt%d" % i, [128, D], F32), S.tt("xt%d" % i)) for i in range(3)])
            junk = sb(p0, "junk", [128, D], BF16); T_junk = S.tt("junk")
            xnbs = Rot([(sb(p0, "xnb%d" % i, [128, D], BF16), S.tt("xnb%d" % i)) for i in range(2)])
            ss = sb(p0, "ss", [128, NT], F32); T_ss = [S.tt("ss%d" % i) for i in range(NT)]
            sq = sb(p0, "sq", [128, NT], F32); T_sq = [S.tt("sq%d" % i) for i in range(NT)]
            rs = sb(p0, "rs", [128, NT], F32); T_rs = [S.tt("rs%d" % i) for i in range(NT)]
            ptr = Rot([(ps(p0, "ptr%d" % i, [128, 8, 128], BF16), S.tt("ptr%d" % i)) for i in range(2)])
            for i in range(NT):
                xt, T_xt = xts.next()
                S.dma("sp", xt[:], x_d[i * 128:(i + 1) * 128, :], [T_in], [T_xt])
                S.add("act", lambda e, xt=xt, i=i: e.activation(out=junk[:], in_=xt[:], func=AF.Square,
                                                                 accum_out=ss[:, i:i + 1]),
                      [T_xt], [T_junk, T_ss[i]])
                S.add("act", lambda e, i=i: e.activation(out=sq[:, i:i + 1], in_=ss[:, i:i + 1], func=AF.Sqrt,
                                                         scale=1.0 / D, bias=EPS),
                      [T_ss[i]], [T_sq[i]])
                S.add("dve", lambda e, i=i: e.reciprocal(out=rs[:, i:i + 1], in_=sq[:, i:i + 1]), [T_sq[i]], [T_rs[i]])
                xnb, T_xnb = xnbs.next()
                S.add("dve", lambda e, xt=xt, xnb=xnb, i=i: e.tensor_scalar(xnb[:], xt[:], rs[:, i:i + 1], None, ALU.mult),
                      [T_xt, T_rs[i]], [T_xnb])
                pt, T_pt = ptr.next()
                for j in range(8):
                    S.add("pe", lambda e, pt=pt, xnb=xnb, j=j: e.transpose(pt[:, j, :], xnb[:, j * 128:(j + 1) * 128], identb[:]),
                          [T_xnb, T_identb], [T_pt])
                S.add("act", lambda e, pt=pt, i=i: e.activation(out=xnT[:, :, i * 128:(i + 1) * 128], in_=pt[:], func=AF.Copy),
                      [T_pt], [T_xnT[i]])
            for j in range(8):
                S.dma("pool", xnT_d[j * 128:(j + 1) * 128, :], xnT[:, j, :], T_xnT, [T_xnT_d])
            if "xnT" in dbg_d:
                for j in range(8):
                    S.dma("pool", dbg_d["xnT"][j * 128:(j + 1) * 128, :], xnT[:, j, :], T_xnT, [T_y])
            S.barrier()

        w_in_v = w_in_d.rearrange("(j p) n -> p j n", p=128)
        repA_d = nc.dram_tensor("repA_scr", [24, 128, LA], F32, kind="Internal").ap()

        def load_w_in(stack_bufs, c0, ncols, dst, T_dst, dcol=0, xsrc=None):
            stg, T_stg = stack_bufs.next()
            S.dma("sp", stg[:, :, 0:ncols], w_in_v[:, :, c0:c0 + ncols], [T_in], [T_stg])
            for j in range(8):
                if j % 2 == 0:
                    S.add("dve", lambda e, j=j: e.tensor_scalar(dst[:, j, dcol:dcol + ncols], stg[:, j, 0:ncols], g1T[:, j:j + 1], None, ALU.mult),
                          [T_stg, T_g1T], [T_dst])
                else:
                    S.add("act", lambda e, j=j: e.activation(out=dst[:, j, dcol:dcol + ncols], in_=stg[:, j, 0:ncols], func=AF.Copy, scale=g1T[:, j:j + 1]),
                          [T_stg, T_g1T], [T_dst])

        def proj512(pp, T_pp, wb, T_wb, c, xn=None, T_xn=None):
            xn = xnT if xn is None else xn
            T_xn = T_xnT[4 * c:4 * c + 4] if T_xn is None else T_xn
            for j in range(8):
                S.add("pe", lambda e, j=j: e.matmul(pp[:], wb[:, j, :], xn[:, j, c * 512:(c + 1) * 512],
                                                     start=(j == 0), stop=(j == 7)),
                      [T_wb] + T_xn, [T_pp])


        repS_d = nc.dram_tensor("repS_scr", [8, 128, LS], F32, kind="Internal").ap()
        repW_d = nc.dram_tensor("repW_scr", [8, 128, LW], F32, kind="Internal").ap()
        w1_v = {"k": kw1_d.rearrange("l d h -> d l h"), "v": vw1_d.rearrange("l d h -> d l h")}

        def build_B():
          with contextlib.ExitStack() as pb:
            QM = [sb(pb, "QM%d" % r, [128, SEQ], BF16) for r in range(4)]
            T_QMq = [S.tt("QMq%d" % r) for r in range(4)]
            T_QMn = [[S.tt("QMn%d_%d" % (r, c)) for c in range(8)] for r in range(4)]
            kslE = sb(pb, "kslE", [128, SEQ], BF16); T_kslE = S.tt("kslE")
            kwz = sb(pb, "kwz", [128, SEQ], BF16); T_kwz = S.tt("kwz")
            vsl_aug = sb(pb, "vsl_aug", [128, 32, 128], BF16); T_vsl = S.tt("vsl_aug")
            vw_aug = sb(pb, "vw_aug", [128, 32, 128], BF16); T_vw = S.tt("vw_aug")
            kcmpz = sb(pb, "kcmpz", [128, 256], BF16); T_kcmpz = S.tt("kcmpz")
            vcmp_aug = sb(pb, "vcmp_aug", [128, 2, 128], BF16); T_vcmp = S.tt("vcmp_aug")
            gT = sb(pb, "gT", [128, SEQ], F32); T_gT = S.tt("gT")
            MC = sb(pb, "MC", [128, 2560], BF16); T_MC = S.tt("MC")
            selc = sb(pb, "selc", [128, 12, 128], F32); T_selc = S.tt("selc")
            keepS = sb(pb, "keepS", [128, 126], F32); T_keepS = S.tt("keepS")
            addS = sb(pb, "addS", [128, 126], F32); T_addS = S.tt("addS")
            mimp = sb(pb, "mimp", [128, 2, 128], BF16); T_mimp = S.tt("mimp")
            S.dma("sp", MC[:], c_mc_d[:, :], [T_in], [T_MC])
            S.dma("sp", selc[:], c_sel_d[:, :, :], [T_in], [T_selc])
            S.dma("sp", keepS[:], c_keep_d[:, :], [T_in], [T_keepS])
            S.dma("sp", addS[:], c_add_d[:, :], [T_in], [T_addS])
            S.dma("sp", mimp[:], c_mimp_d[:, :, :], [T_in], [T_mimp])
            S.dma("sp", kslE[64:128, :], c_ex_d[:, :], [T_in], [T_kslE])
            S.add("dve", lambda e: e.memset(kwz[64:128, :], 0.0), [], [T_kwz])
            for r in range(4):
                S.add("dve", lambda e, r=r: e.memset(QM[r][64:128, :], 0.0), [], T_QMn[r])
            S.add("dve", lambda e: e.memset(vsl_aug[:, :, 0:64], 1.0), [], [T_vsl])
            S.add("dve", lambda e: e.memset(vw_aug[:, :, 0:64], 1.0), [], [T_vw])
            S.add("dve", lambda e: e.memset(vcmp_aug[:, :, 0:64], 1.0), [], [T_vcmp])

            for g in range(2):
                with contextlib.ExitStack() as pj:
                    xnb_ = sb(pj, "xnTb", [128, 8, SEQ], BF16); T_xnb = S.tt("xnTb%d" % g)
                    for j in range(8):
                        S.dma("sp", xnb_[:, j, :], xnT_d[j * 128:(j + 1) * 128, :], [T_xnT_d], [T_xnb])
                    wst = Rot([(sb(pj, "wstB", [128, 8, 128], F32), S.tt("wstB%d" % g))])
                    wbs = Rot([(sb(pj, "wbB%d" % i, [128, 8, 128], BF16), S.tt("wbB%d_%d" % (g, i))) for i in range(2)])
                    tmpT = sb(pj, "tmpT", [128, SEQ], BF16); T_tmpT = S.tt("tmpT%d" % g)
                    w1z = sb(pj, "w1z", [128, 32, 128], BF16); T_w1z = S.tt("w1z%d" % g)
                    w2t = sb(pj, "w2t", [128, 128], BF16); T_w2t = S.tt("w2t%d" % g)
                    w2s = sb(pj, "w2s", [128, 64], F32); T_w2s = S.tt("w2s%d" % g)
                    posT2 = sb(pj, "posT2", [128, 32], BF16); T_posT2 = S.tt("posT2_%d" % g)
                    poss = sb(pj, "poss", [128, 32], F32); T_poss = S.tt("poss%d" % g)
                    b1c = sb(pj, "b1c", [128, 1], F32); T_b1c = S.tt("b1c%d" % g)
                    cb = sb(pj, "cb", [128, 1], F32); T_cb = S.tt("cb%d" % g)
                    zt = sb(pj, "zt", [128, 256], F32); T_zt = S.tt("zt%d" % g)
                    ut = sb(pj, "ut", [128, 256], F32); T_ut = S.tt("ut%d" % g)
                    hidT = sb(pj, "hidT", [128, 256], BF16); T_hidT = S.tt("hidT%d" % g)
                    vtmp = Rot([(sb(pj, "vtmpB%d" % i, [128, 8, 128], BF16), S.tt("vtmpB%d_%d" % (g, i))) for i in range(2)])
                    ppj = Rot([(ps(pj, "ppjB%d" % i, [128, 512], F32), S.tt("ppjB%d_%d" % (g, i))) for i in range(2)])
                    phid = ps(pj, "phid", [128, 256], F32); T_phid = S.tt("phid%d" % g)
                    pcb = ps(pj, "pcb", [128, 8], F32); T_pcb = S.tt("pcb%d" % g)
                    ptrB = ps(pj, "ptrB", [128, 8, 128], BF16); T_ptrB = S.tt("ptrB%d" % g)
                    Txc = lambda c: [T_xnb]

                    def proj_pair(colspecs, evac):
                        wb, T_wb = wbs.next()
                        if sum(n_ for _, n_, _ in colspecs) < 128:
                            S.add("dve", lambda e, wb=wb: e.memset(wb[:], 0.0), [], [T_wb])
                        for c0, n_, dcol in colspecs:
                            load_w_in(wst, c0, n_, wb, T_wb, dcol=dcol)
                        for c in range(8):
                            pp, T_pp = ppj.next()
                            proj512(pp, T_pp, wb, T_wb, c, xn=xnb_, T_xn=[T_xnb])
                            evac(pp, T_pp, c)

                    for pair in range(2):
                        def evq(pp, T_pp, c, pair=pair):
                            S.add("act", lambda e: e.activation(out=QM[2 * pair][0:64, c * 512:(c + 1) * 512], in_=pp[0:64, :],
                                                                func=AF.Copy, scale=0.125), [T_pp], [T_QMq[2 * pair]])
                            S.add("dve", lambda e: e.tensor_scalar(QM[2 * pair + 1][0:64, c * 512:(c + 1) * 512], pp[64:128, :], 0.125, None, ALU.mult),
                                  [T_pp], [T_QMq[2 * pair + 1]])
                        proj_pair([(1536 + g * 256 + pair * 128, 128, 0)], evq)
                    def evk(pp, T_pp, c):
                        S.add("act", lambda e: e.activation(out=kslE[0:64, c * 512:(c + 1) * 512], in_=pp[0:64, :], func=AF.Copy), [T_pp], [T_kslE])
                        S.add("dve", lambda e: e.tensor_copy(kwz[0:64, c * 512:(c + 1) * 512], pp[64:128, :]), [T_pp], [T_kwz])
                    proj_pair([(2304 + g * 64, 64, 0), (2560 + g * 64, 64, 64)], evk)
                    def evg(pp, T_pp, c):
                        S.add("act", lambda e: e.activation(out=gT[:, c * 512:(c + 1) * 512], in_=pp[:], func=AF.Tanh, scale=0.5), [T_pp], [T_gT])
                        S.add("dve", lambda e: e.tensor_scalar(gT[:, c * 512:(c + 1) * 512], gT[:, c * 512:(c + 1) * 512], 0.5, 0.5, ALU.mult, ALU.add),
                              [T_gT], [T_gT])
                    proj_pair([(2816 + g * 12, 12, 0)], evg)
                    def evv(pp, T_pp, c):
                        S.add("act", lambda e: e.activation(out=tmpT[:, c * 512:(c + 1) * 512], in_=pp[:], func=AF.Copy), [T_pp], [T_tmpT])
                    proj_pair([(2432 + g * 64, 64, 0), (2688 + g * 64, 64, 64)], evv)
                    for m0 in range(0, 32, 8):
                        for mm in range(8):
                            m = m0 + mm
                            S.add("pe", lambda e, mm=mm, m=m: e.transpose(ptrB[:, mm, :], tmpT[:, m * 128:(m + 1) * 128], identb[:]),
                                  [T_tmpT, T_identb], [T_ptrB])
                        vt, T_vt = vtmp.next()
                        S.add("act", lambda e, vt=vt: e.activation(out=vt[:], in_=ptrB[:], func=AF.Copy), [T_ptrB], [T_vt])
                        S.add("dve", lambda e, vt=vt, m0=m0: e.tensor_copy(vsl_aug[:, m0:m0 + 8, 64:128], vt[:, :, 0:64]), [T_vt], [T_vsl])
                        S.add("dve", lambda e, vt=vt, m0=m0: e.tensor_copy(vw_aug[:, m0:m0 + 8, 64:128], vt[:, :, 64:128]), [T_vt], [T_vw])
                    proj_pair([(2048 + g * 64, 64, 0), (2176 + g * 64, 64, 64)], evv)
                    for which, w1v_, b1_d, w2_d, rows in (("k", w1_v["k"], kb1_d, kw2_d, slice(0, 64)), ("v", w1_v["v"], vb1_d, vw2_d, slice(64, 128))):
                        S.add("dve", lambda e: e.memset(w1z[:], 0.0), [], [T_w1z])
                        for l0 in range(0, 32, 8):
                            stg, T_stg = wst.next()
                            S.dma("sp", stg[rows, :, :], w1v_[:, l0:l0 + 8, :], [T_in], [T_stg])
                            S.add("dve", lambda e, stg=stg, l0=l0, rows=rows: e.tensor_copy(w1z[rows, l0:l0 + 8, :], stg[rows, :, :]), [T_stg], [T_w1z])
                        S.dma("sp", poss[rows, :], pos_d.rearrange("l d -> d l"), [T_in], [T_poss], allow_slow_non_contiguous=True)
                        S.add("dve", lambda e: e.memset(posT2[:], 0.0), [], [T_posT2])
                        S.add("dve", lambda e, rows=rows: e.tensor_copy(posT2[rows, :], poss[rows, :]), [T_poss], [T_posT2])
                        S.dma("sp", b1c[:], b1_d.rearrange("o h -> h o"), [T_in], [T_b1c], allow_slow_non_contiguous=True)
                        for l in range(32):
                            S.add("pe", lambda e, l=l: e.matmul(pcb[:, 0:1], w1z[:, l, :], posT2[:, l:l + 1], start=(l == 0), stop=(l == 31)),
                                  [T_w1z, T_posT2], [T_pcb])
                        S.add("dve", lambda e: e.tensor_tensor(cb[:], pcb[:, 0:1], b1c[:], ALU.add), [T_pcb, T_b1c], [T_cb])
                        for l in range(32):
                            S.add("pe", lambda e, l=l: e.matmul(phid[:, 0:255], w1z[:, l, :], tmpT[:, sl(l, 255, 16)], start=(l == 0), stop=(l == 31)),
                                  [T_w1z, T_tmpT], [T_phid])
                        S.add("act", lambda e: e.activation(out=zt[:, 0:255], in_=phid[:, 0:255], func=AF.Identity, bias=cb[:]), [T_phid, T_cb], [T_zt])
                        S.add("dve", lambda e: e.tensor_tensor(ut[:, 0:255], zt[:, 0:255], zt[:, 0:255], ALU.mult), [T_zt], [T_ut])
                        S.add("dve", lambda e: e.tensor_scalar(ut[:, 0:255], ut[:, 0:255], 0.044715, 1.0, ALU.mult, ALU.add), [T_ut], [T_ut])
                        S.add("dve", lambda e: e.tensor_tensor(ut[:, 0:255], ut[:, 0:255], zt[:, 0:255], ALU.mult), [T_ut, T_zt], [T_ut])
                        S.add("act", lambda e: e.activation(out=ut[:, 0:255], in_=ut[:, 0:255], func=AF.Tanh, scale=0.7978845608028654), [T_ut], [T_ut])
                        S.add("dve", lambda e: e.tensor_scalar(ut[:, 0:255], ut[:, 0:255], 0.5, 0.5, ALU.mult, ALU.add), [T_ut], [T_ut])
                        S.add("dve", lambda e: e.memset(hidT[:], 0.0), [], [T_hidT])
                        S.add("dve", lambda e: e.tensor_tensor(hidT[:, 0:255], ut[:, 0:255], zt[:, 0:255], ALU.mult), [T_ut, T_zt], [T_hidT])
                        S.dma("sp", w2s[:], w2_d[:, :], [T_in], [T_w2s])
                        S.add("dve", lambda e: e.memset(w2t[:], 0.0), [], [T_w2t])
                        if which == "k":
                            S.add("dve", lambda e: e.tensor_copy(w2t[:, 0:64], w2s[:]), [T_w2s], [T_w2t])
                            S.add("pe", lambda e: e.matmul(phid[:, 0:256], w2t[:], hidT[:], start=True, stop=True), [T_w2t, T_hidT], [T_phid])
                            S.add("act", lambda e: e.activation(out=kcmpz[:], in_=phid[:, 0:256], func=AF.Copy), [T_phid], [T_kcmpz])
                        else:
                            S.add("dve", lambda e: e.tensor_copy(w2t[:, 64:128], w2s[:]), [T_w2s], [T_w2t])
                            for t in range(2):
                                S.add("pe", lambda e, t=t: e.matmul(phid[:, t * 128:(t + 1) * 128], hidT[:, t * 128:(t + 1) * 128], w2t[:], start=True, stop=True),
                                      [T_w2t, T_hidT], [T_phid])
                            S.add("act", lambda e: e.activation(out=ut[:, :], in_=phid[:, 0:256], func=AF.Copy), [T_phid], [T_ut])
                            S.add("dve", lambda e: e.tensor_copy(vcmp_aug[:, :, 64:128], ut[:].rearrange("p (t c) -> p t c", t=2)[:, :, 64:128]),
                                  [T_ut], [T_vcmp])
                    S.barrier()

                with contextlib.ExitStack() as pl:
                    Ws = [sb(pl, "Ws%d" % r, [128, 2560], BF16) for r in range(4)]
                    Ww = [sb(pl, "Ww%d" % r, [128, 1408], BF16) for r in range(4)]
                    T_Ws = [S.tt("Ws%d_%d" % (g, r)) for r in range(4)]
                    T_Ww = [S.tt("Ww%d_%d" % (g, r)) for r in range(4)]
                    ohc = Rot([(sb(pl, "ohc%d" % i, [33, 512], F32), S.tt("ohc%d_%d" % (g, i))) for i in range(2)])
                    tabrep = Rot([(sb(pl, "tabrB%d" % i, [33, 128], F32), S.tt("tabrB%d_%d" % (g, i))) for i in range(2)])
                    repc = Rot([(sb(pl, "repc%d" % i, [128, 512], F32), S.tt("repc%d_%d" % (g, i))) for i in range(2)])
                    stgs = Rot([(sb(pl, "stgS%d" % i, [128, 512], F32), S.tt("stgS%d_%d" % (g, i))) for i in range(2)])
                    pp2 = Rot([(ps(pl, "ppL%d" % i, [128, 512], F32), S.tt("ppL%d_%d" % (g, i))) for i in range(2)])
                    psc = Rot([(ps(pl, "pscL%d" % i, [128, 512], F32), S.tt("pscL%d_%d" % (g, i))) for i in range(2)])
                    ppv = Rot([(ps(pl, "ppvL%d" % i, [128, 512], F32), S.tt("ppvL%d_%d" % (g, i))) for i in range(2)])
                    pip = ps(pl, "pipL", [128, 512], F32); T_pip = S.tt("pipL%d" % g)
                    ptn = ps(pl, "ptnL", [128, 4, 128], BF16); T_ptn = S.tt("ptnL%d" % g)
                    T_repd = {}
                    for r in range(4):
                        hcol = 8 + 4 * g + r
                        slot = 4 * g + r
                        tr_, T_tr = tabrep.next()
                        S.add("dve", lambda e, tr_=tr_, hcol=hcol: e.tensor_scalar(tr_[:, :], zer33[:, :], tab[:, hcol:hcol + 1], None, ALU.add),
                              [T_tab, T_zer33], [T_tr])
                        for kind, L, X, rep_d, dst, T_dst in (("s", LS, 2560, repS_d, Ws[r], T_Ws[r]), ("w", LW, 1408, repW_d, Ww[r], T_Ww[r])):
                            T_rd = S.tt("repd_%s%d" % (kind, slot))
                            for i0 in range(0, L, 512):
                                n_ = min(512, L - i0)
                                oc, T_oc = ohc.next()
                                S.dma("sp", oc[:, 0:n_], c_oh_d[:, OH_OFF[kind] + i0:OH_OFF[kind] + i0 + n_], [T_in], [T_oc])
                                pb_, T_pb = pp2.next()
                                S.add("pe", lambda e, pb_=pb_, tr_=tr_, oc=oc, n_=n_: e.matmul(pb_[:, 0:n_], tr_[:, :], oc[:, 0:n_], start=True, stop=True),
                                      [T_tr, T_oc], [T_pb])
                                rc, T_rc = repc.next()
                                S.add("act", lambda e, pb_=pb_, rc=rc, n_=n_: e.activation(out=rc[:, 0:n_], in_=pb_[:, 0:n_], func=AF.Copy), [T_pb], [T_rc])
                                S.dma("pool", rep_d[slot, :, i0:i0 + n_], rc[:, 0:n_], [T_rc], [T_rd])
                            for x0 in range(0, X, 512):
                                n_ = min(512, X - x0)
                                sg_, T_sg = stgs.next()
                                src = bass.AP(rep_d.tensor, slot * 128 * L + 127 + x0, [[L - 1, 128], [1, n_]])
                                S.dma("sp", sg_[:, 0:n_], src, [T_rd], [T_sg])
                                S.add("act", lambda e, sg_=sg_, dst=dst, x0=x0, n_=n_: e.activation(out=dst[:, x0:x0 + n_], in_=sg_[:, 0:n_], func=AF.Exp),
                                      [T_sg], [T_dst])

                    exs = Rot([(sb(pl, "exL%d" % i, [128, 512], BF16), S.tt("exL%d_%d" % (g, i))) for i in range(5)])
                    pts = Rot([(sb(pl, "ptL%d" % i, [128, 512], BF16), S.tt("ptL%d_%d" % (g, i))) for i in range(4)])
                    grep_ = Rot([(sb(pl, "grep%d" % i, [128, 3, 512], F32), S.tt("grep%d_%d" % (g, i))) for i in range(4)])
                    rdt = Rot([(sb(pl, "rdL%d" % i, [128, 512], F32), S.tt("rdL%d_%d" % (g, i))) for i in range(2)])
                    fts = Rot([(sb(pl, "ftL%d" % i, [128, 512], F32), S.tt("ftL%d_%d" % (g, i))) for i in range(2)])
                    tms = Rot([(sb(pl, "tmL%d" % i, [128, 512], F32), S.tt("tmL%d_%d" % (g, i))) for i in range(2)])
                    acc = [sb(pl, "accL%d" % r, [128, 512], F32) for r in range(4)]
                    T_acc = [S.tt("accL%d_%d" % (g, r)) for r in range(4)]
                    impacc = sb(pl, "impacc", [128, 512], F32); T_impacc = S.tt("impacc%d" % g)
                    imod = sb(pl, "imod", [128, 4, 64], F32); T_imod = S.tt("imod%d" % g)
                    wrk = sb(pl, "wrk", [128, 64], F32); T_wrk = S.tt("wrk%d" % g)
                    m8a = sb(pl, "m8a", [128, 8], F32); T_m8a = S.tt("m8a%d" % g)
                    m8b = sb(pl, "m8b", [128, 8], F32); T_m8b = S.tt("m8b%d" % g)
                    negm = sb(pl, "negm", [128, 4, 128], BF16); T_negm = S.tt("negm%d" % g)
                    ntmp = sb(pl, "ntmp", [128, 512], BF16); T_ntmp = S.tt("ntmp%d" % g)
                    stage = Rot([(sb(pl, "stgO%d" % i, [128, 512], BF16), S.tt("stgO%d_%d" % (g, i))) for i in range(2)])
                    ptr32 = pip
                    S.add("dve", lambda e: e.memset(impacc[:], 0.0), [], [T_impacc])
                    S.add("dve", lambda e: e.memset(negm[:], 0.0), [], [T_negm])

                    def attend(r, c, kts, kT, T_kT, vaug, T_va, strip, T_strip, x0_of, first_mask=None):
                        pv, T_pv = ppv.next()
                        n_t = len(kts)
                        Ps = [None] * n_t
                        LA_ = 2
                        for ii in range(n_t + LA_):
                            if ii < n_t:
                                kt = kts[ii]
                                sc, T_sc = psc.next()
                                S.add("pe", lambda e, sc=sc, kt=kt: e.matmul(sc[:], kT[:, kt * 128:(kt + 1) * 128], QM[r][:, c * 512:(c + 1) * 512],
                                                                             start=True, stop=True), [T_kT, T_QMq[r], T_QMn[r][c]], [T_sc])
                                ex, T_ex = exs.next()
                                S.add("act", lambda e, sc=sc, ex=ex: e.activation(out=ex[:], in_=sc[:], func=AF.Exp), [T_sc], [T_ex])
                                x0 = x0_of(kt)
                                if x0 is None:
                                    P_, T_P = ex, T_ex
                                else:
                                    P_, T_P = pts.next()
                                    S.add("dve", lambda e, ex=ex, P_=P_, x0=x0: e.tensor_tensor(P_[:], ex[:], strip[:, x0:x0 + 512], ALU.mult),
                                          [T_ex, T_strip], [T_P])
                                Ps[ii] = (P_, T_P)
                            jj = ii - LA_
                            if jj >= 0:
                                kt = kts[jj]
                                P_, T_P = Ps[jj]
                                S.add("pe", lambda e, pv=pv, kt=kt, P_=P_, jj=jj: e.matmul(pv[:], vaug[:, kt, :], P_[:], start=(jj == 0), stop=(jj == n_t - 1)),
                                      [T_va, T_P], [T_pv])
                                if first_mask is not None:
                                    first_mask(kt, P_, T_P, jj, n_t)
                        return pv, T_pv

                    for c in range(8):
                        q0 = c * 512
                        cs = slice(q0, q0 + 512)
                        for r in range(4):
                            gr, T_gr = grep_.next()
                            for br in range(3):
                                pg, T_pg = pp2.next()
                                S.add("pe", lambda e, pg=pg, r=r, br=br: e.matmul(pg[:], selc[:, r * 3 + br, :], gT[:, cs], start=True, stop=True),
                                      [T_selc, T_gT], [T_pg])
                                S.add("act", lambda e, pg=pg, gr=gr, br=br: e.activation(out=gr[0:64, br, :], in_=pg[0:64, :], func=AF.Copy), [T_pg], [T_gr])
                            nt = 1 if c < 4 else 2

                            def impmm(kt, P_, T_P, ii, n_, r=r):
                                S.add("pe", lambda e: e.matmul(pip[:], mimp[:, kt, :], P_[:], start=(ii == 0), stop=(ii == n_ - 1)),
                                      [T_mimp, T_P], [T_pip])
                            pv, T_pv = attend(r, c, list(range(nt)), kcmpz, T_kcmpz, vcmp_aug, T_vcmp, MC, T_MC,
                                              lambda kt: (q0 - 2048 * kt) if (q0 - 2048 * kt) < 2560 else None, first_mask=impmm)
                            rd, T_rd = rdt.next()
                            S.add("dve", lambda e, pv=pv, rd=rd: e.tensor_scalar(rd[0:64, :], pv[0:64, :], 1e-30, None, ALU.max), [T_pv], [T_rd])
                            S.add("dve", lambda e, rd=rd: e.reciprocal(out=rd[0:64, :], in_=rd[0:64, :]), [T_rd], [T_rd])
                            if r == 0:
                                S.add("dve", lambda e, rd=rd: e.tensor_tensor(impacc[0:64, :], pip[0:64, :], rd[0:64, :], ALU.mult), [T_pip, T_rd], [T_impacc])
                            else:
                                tm, T_tm = tms.next()
                                S.add("dve", lambda e, rd=rd, tm=tm: e.tensor_tensor(tm[0:64, :], pip[0:64, :], rd[0:64, :], ALU.mult), [T_pip, T_rd], [T_tm])
                                S.add("dve", lambda e, tm=tm: e.tensor_tensor(impacc[0:64, :], impacc[0:64, :], tm[0:64, :], ALU.add), [T_impacc, T_tm], [T_impacc])
                            ft, T_ft = fts.next()
                            S.add("dve", lambda e, gr=gr, rd=rd, ft=ft: e.tensor_tensor(ft[0:64, :], gr[0:64, 0, :], rd[0:64, :], ALU.mult), [T_gr, T_rd], [T_ft])
                            S.add("dve", lambda e, pv=pv, ft=ft, r=r: e.tensor_tensor(acc[r][0:64, :], pv[64:128, :], ft[0:64, :], ALU.mult),
                                  [T_pv, T_ft], [T_acc[r]])
                            if r == 0:
                                greps = []
                            greps.append((gr, T_gr))
                        for qt in range(4):
                            S.add("pe", lambda e, qt=qt: e.transpose(ptr32[:, qt * 128:(qt + 1) * 128], impacc[:, qt * 128:(qt + 1) * 128], identf[:]),
                                  [T_impacc, T_identf], [T_pip])
                        for qt in range(4):
                            i_ = 4 * c + qt
                            y0 = 62 - 2 * i_
                            S.add("dve", lambda e, qt=qt, y0=y0: e.tensor_tensor(imod[:, qt, :], ptr32[:, qt * 128:qt * 128 + 64], keepS[:, y0:y0 + 64], ALU.mult),
                                  [T_pip, T_keepS], [T_imod])
                            S.add("dve", lambda e, qt=qt, y0=y0: e.tensor_tensor(imod[:, qt, :], imod[:, qt, :], addS[:, y0:y0 + 64], ALU.add),
                                  [T_imod, T_addS], [T_imod])
                        S.add("dve", lambda e: e.memset(imod[:, :, 0:1], 1e30), [], [T_imod])
                        for qt in range(4):
                            S.add("dve", lambda e, qt=qt: e.max(out=m8a[:], in_=imod[:, qt, :]), [T_imod], [T_m8a])
                            S.add("dve", lambda e, qt=qt: e.match_replace(out=wrk[:], in_to_replace=m8a[:], in_values=imod[:, qt, :], imm_value=-2e30),
                                  [T_imod, T_m8a], [T_wrk])
                            S.add("dve", lambda e: e.max(out=m8b[:], in_=wrk[:]), [T_wrk], [T_m8b])
                            S.add("dve", lambda e, qt=qt: e.tensor_scalar(negm[:, qt, 64:128], imod[:, qt, :], m8b[:, 7:8], NEG, ALU.is_lt, ALU.mult),
                                  [T_imod, T_m8b], [T_negm])
                        for qt in range(4):
                            S.add("pe", lambda e, qt=qt: e.transpose(ptn[:, qt, :], negm[:, qt, :], identb[:]), [T_negm, T_identb], [T_ptn])
                        S.add("act", lambda e: e.activation(out=ntmp[:], in_=ptn[:].rearrange("p a b -> p (a b)"), func=AF.Copy), [T_ptn], [T_ntmp])
                        for r in range(4):
                            S.add("dve", lambda e, r=r: e.tensor_copy(QM[r][64:128, cs], ntmp[64:128, :]), [T_ntmp], [T_QMn[r][c]])
                        for r in range(4):
                            gr, T_gr = greps[r]
                            for bi, (kT, T_kT, vaug, T_va, strip, T_strip, kts, x0f) in enumerate((
                                    (kslE, T_kslE, vsl_aug, T_vsl, Ws[r], T_Ws[r], list(range(0, 4 * c + 4)),
                                     lambda kt: min(q0 - kt * 128 + 384, 2048)),
                                    (kwz, T_kwz, vw_aug, T_vw, Ww[r], T_Ww[r], list(range(max(0, 4 * c - 4), 4 * c + 4)),
                                     lambda kt: q0 - kt * 128 + 384))):
                                pv, T_pv = attend(r, c, kts, kT, T_kT, vaug, T_va, strip, T_strip, x0f)
                                rd, T_rd = rdt.next()
                                S.add("dve", lambda e, pv=pv, rd=rd: e.reciprocal(out=rd[0:64, :], in_=pv[0:64, :]), [T_pv], [T_rd])
                                ft, T_ft = fts.next()
                                S.add("dve", lambda e, gr=gr, rd=rd, ft=ft, bi=bi: e.tensor_tensor(ft[0:64, :], gr[0:64, 1 + bi, :], rd[0:64, :], ALU.mult),
                                      [T_gr, T_rd], [T_ft])
                                tm, T_tm = tms.next()
                                S.add("dve", lambda e, pv=pv, ft=ft, tm=tm: e.tensor_tensor(tm[0:64, :], pv[64:128, :], ft[0:64, :], ALU.mult),
                                      [T_pv, T_ft], [T_tm])
                                S.add("dve", lambda e, tm=tm, r=r: e.tensor_tensor(acc[r][0:64, :], acc[r][0:64, :], tm[0:64, :], ALU.add),
                                      [T_acc[r], T_tm], [T_acc[r]])
                            if r % 2 == 0:
                                so, T_so = stage.next()
                            if r % 2 == 0:
                                S.add("act", lambda e, so=so, r=r: e.activation(out=so[0:64, :], in_=acc[r][0:64, :], func=AF.Copy), [T_acc[r]], [T_so])
                            else:
                                S.add("act", lambda e, so=so, r=r: e.activation(out=so[64:128, :], in_=acc[r][0:64, :], func=AF.Copy), [T_acc[r]], [T_so])
                                row0 = 512 + g * 256 + (r // 2) * 128
                                S.dma("pool", mixT_d[row0:row0 + 128, cs], so[:], [T_so], [T_mixT_d[4 + 2 * g + r // 2]])
                    S.barrier()

        if "A" in parts:
          with contextlib.ExitStack() as pa:
            EA = sb(pa, "EA", [128, 24, 256], BF16); T_EA = S.tt("EA")
            with contextlib.ExitStack() as pas:
                oha = sb(pas, "oha", [33, 3 * LA], F32); T_oha = S.tt("oha")
                S.dma("sp", oha[:], c_oh_d[:, 0:3 * LA], [T_in], [T_oha])
                tabrep = Rot([(sb(pas, "tabrep%d" % i, [33, 128], F32), S.tt("tabrep%d" % i)) for i in range(2)])
                repsb = Rot([(sb(pas, "repsb%d" % i, [128, 3 * LA], F32), S.tt("repsb%d" % i)) for i in range(2)])
                stg = Rot([(sb(pas, "stgA%d" % i, [128, 3, 256], F32), S.tt("stgA%d" % i)) for i in range(2)])
                pbs = Rot([(ps(pas, "pbs%d" % i, [128, 512], F32), S.tt("pbs%d" % i)) for i in range(2)])
                T_repA_d = [S.tt("repA_d%d" % i) for i in range(8)]
                for hh in range(8):
                    tr_, T_tr = tabrep.next()
                    S.add("dve", lambda e, tr_=tr_, hh=hh: e.tensor_scalar(tr_[:, :], zer33[:, :], tab[:, hh:hh + 1], None, ALU.add),
                          [T_tab, T_zer33], [T_tr])
                    rp, T_rp = repsb.next()
                    for p in range(3):
                        pb, T_pb = pbs.next()
                        S.add("pe", lambda e, pb=pb, tr_=tr_, p=p: e.matmul(pb[:, 0:LA], tr_[:, :], oha[:, p * LA:(p + 1) * LA],
                                                                        start=True, stop=True), [T_tr, T_oha], [T_pb])
                        S.add("act", lambda e, pb=pb, rp=rp, p=p: e.activation(out=rp[:, p * LA:(p + 1) * LA], in_=pb[:, 0:LA], func=AF.Copy),
                              [T_pb], [T_rp])
                    S.dma("pool", repA_d[hh * 3:(hh + 1) * 3].rearrange("v k l -> k v l"), rp[:].rearrange("k (v l) -> k v l", v=3),
                          [T_rp], [T_repA_d[hh]])
                    sg_, T_sg = stg.next()
                    for p in range(3):
                        src = bass.AP(repA_d.tensor, (hh * 3 + p) * 128 * LA + 127, [[LA - 1, 128], [1, 256]])
                        S.dma("sp", sg_[:, p, :], src, [T_repA_d[hh]], [T_sg])
                    S.add("act", lambda e, sg_=sg_, hh=hh: e.activation(out=EA[:, hh * 3:(hh + 1) * 3, :], in_=sg_[:], func=AF.Exp),
                          [T_sg], [T_EA])
                S.barrier()
            if "EA" in dbg_d:
                S.dma("pool", dbg_d["EA"].rearrange("v k x -> k v x"), EA[:], [T_EA], [T_y])

            wst = Rot([(sb(pa, "wst%d" % i, [128, 8, 128], F32), S.tt("wst%d" % i)) for i in range(2)])
            wq = sb(pa, "wq", [128, 8, 128], BF16); T_wq = S.tt("wq")
            wk = sb(pa, "wk", [128, 8, 128], BF16); T_wk = S.tt("wk")
            wv = sb(pa, "wv", [128, 8, 128], BF16); T_wv = S.tt("wv")
            aqT = sb(pa, "aqT", [128, SEQ], BF16); T_aqT = S.tt("aqT")
            akz = [sb(pa, "akz%d" % l, [128, SEQ], BF16) for l in range(2)]
            T_akz = [S.tt("akz%d" % l) for l in range(2)]
            avTp = sb(pa, "avTp", [128, SEQ], BF16); T_avTp = S.tt("avTp")
            vblk = [sb(pa, "vblk%d" % l, [128, 32, 128], BF16) for l in range(2)]
            T_vblk = [S.tt("vblk%d" % l) for l in range(2)]
            numacc = [sb(pa, "numacc%d" % l, [128, SEQ], F32) for l in range(2)]
            T_numacc = [S.tt("numacc%d" % l) for l in range(2)]
            rdt = sb(pa, "rdt", [128, 512], F32); T_rdt = S.tt("rdt")
            mst = Rot([(sb(pa, "mst%d" % i, [128, 1024], BF16), S.tt("mst%d" % i)) for i in range(2)])
            vtmp = Rot([(sb(pa, "vtmp%d" % i, [128, 8, 128], BF16), S.tt("vtmp%d" % i)) for i in range(2)])
            exs = Rot([(sb(pa, "exA%d" % i, [128, 2, 256], BF16), S.tt("exA%d" % i)) for i in range(3)])
            pts = Rot([(sb(pa, "ptA%d" % i, [128, 2, 256], BF16), S.tt("ptA%d" % i)) for i in range(4)])
            ppj = Rot([(ps(pa, "ppj%d" % i, [128, 512], F32), S.tt("ppj%d" % i)) for i in range(2)])
            ptr = Rot([(ps(pa, "ptrA%d" % i, [128, 8, 128], BF16), S.tt("ptrA%d" % i)) for i in range(1)])
            psc = Rot([(ps(pa, "pscA%d" % i, [128, 2, 256], F32), S.tt("pscA%d" % i)) for i in range(3)])
            ppv = Rot([(ps(pa, "ppvA%d" % i, [128, 512], F32), S.tt("ppvA%d" % i)) for i in range(2)])
            S.add("dve", lambda e: e.memset(vblk[0][:, :, 64:128], 1.0), [], [T_vblk[0]])
            S.add("dve", lambda e: e.memset(vblk[1][:, :, 0:64], 1.0), [], [T_vblk[1]])
            S.add("dve", lambda e: e.memset(akz[0][64:128, :], 0.0), [], [T_akz[0]])
            S.add("dve", lambda e: e.memset(akz[1][0:64, :], 0.0), [], [T_akz[1]])

            for hp in range(4 if A_STOP is None else 1):
                load_w_in(wst, hp * 128, 128, wq, T_wq)
                load_w_in(wst, 512 + hp * 128, 128, wk, T_wk)
                load_w_in(wst, 1024 + hp * 128, 128, wv, T_wv)
                for c in range(8):
                    pp, T_pp = ppj.next()
                    proj512(pp, T_pp, wq, T_wq, c)
                    S.add("act", lambda e, pp=pp, c=c: e.activation(out=aqT[:, c * 512:(c + 1) * 512], in_=pp[:], func=AF.Copy, scale=0.125),
                          [T_pp], [T_aqT])
                    pp, T_pp = ppj.next()
                    proj512(pp, T_pp, wk, T_wk, c)
                    S.add("act", lambda e, pp=pp, c=c: e.activation(out=akz[0][0:64, c * 512:(c + 1) * 512], in_=pp[0:64, :], func=AF.Copy),
                          [T_pp], [T_akz[0]])
                    S.add("dve", lambda e, pp=pp, c=c: e.tensor_copy(akz[1][64:128, c * 512:(c + 1) * 512], pp[64:128, :]),
                          [T_pp], [T_akz[1]])
                for p, (win, r) in enumerate(DIL):
                    nb = 32 // r
                    gs = min(4, nb)
                    clen = SEQ // r
                    csz = min(512, clen)
                    for u0 in range(0, SEQ, csz):
                        cres, i0 = u0 // clen, u0 % clen
                        pp, T_pp = ppj.next()
                        for j in range(8):
                            S.add("pe", lambda e, pp=pp, j=j, cres=cres, i0=i0, r=r, csz=csz: e.matmul(
                                pp[:, 0:csz], wv[:, j, :], xnT[:, j, sl(i0 * r + cres, csz, r)], start=(j == 0), stop=(j == 7)),
                                [T_wv] + T_xnT, [T_pp])
                        S.add("act", lambda e, pp=pp, u0=u0, csz=csz: e.activation(out=avTp[:, u0:u0 + csz], in_=pp[:, 0:csz], func=AF.Copy),
                              [T_pp], [T_avTp])
                    for m0 in range(0, 32, 8):
                        pt, T_pt = ptr.next()
                        for mm in range(8):
                            m = m0 + mm
                            S.add("pe", lambda e, pt=pt, mm=mm, m=m: e.transpose(pt[:, mm, :], avTp[:, m * 128:(m + 1) * 128], identb[:]),
                                  [T_avTp, T_identb], [T_pt])
                        vt, T_vt = vtmp.next()
                        S.add("act", lambda e, pt=pt, vt=vt: e.activation(out=vt[:], in_=pt[:], func=AF.Copy), [T_pt], [T_vt])
                        S.add("dve", lambda e, vt=vt, m0=m0: e.tensor_copy(vblk[0][:, m0:m0 + 8, 0:64], vt[:, :, 0:64]), [T_vt], [T_vblk[0]])
                        S.add("dve", lambda e, vt=vt, m0=m0: e.tensor_copy(vblk[1][:, m0:m0 + 8, 64:128], vt[:, :, 64:128]), [T_vt], [T_vblk[1]])
                    if A_STOP == "vblk":
                        continue
                    for l in range(2):
                        hh = 2 * hp + l
                        idx = hh * 3 + p
                        prevP = None
                        pv = T_pv = None
                        pairP = [None] * 16
                        for u in range(16 + 1):
                            if u < 16:
                                sc, T_sc = psc.next()
                                for half in range(2):
                                    m = 2 * u + half
                                    cres, n = m // nb, m % nb
                                    st = 128 * n * r + cres
                                    nq = 256 if n < nb - 1 else 128
                                    S.add("pe", lambda e, sc=sc, half=half, st=st, r=r, nq=nq, l=l: e.matmul(
                                        sc[:, half, 0:nq], akz[l][:, sl(st, 128, r)], aqT[:, sl(st, nq, r)],
                                        start=True, stop=True), [T_akz[l], T_aqT], [T_sc])
                                ex, T_ex = exs.next()
                                S.add("act", lambda e, sc=sc, ex=ex: e.activation(out=ex[:], in_=sc[:], func=AF.Exp), [T_sc], [T_ex])
                                pt_, T_pt_ = pts.next()
                                for hf in range(2):
                                    S.add("dve", lambda e, ex=ex, pt_=pt_, hf=hf, idx=idx: e.tensor_tensor(
                                        pt_[:, hf, :], ex[:, hf, :], EA[:, idx, :], ALU.mult), [T_ex, T_EA], [T_pt_])
                                pairP[u] = (pt_, T_pt_)
                            v = u - 1
                            if v < 0:
                                continue
                            pt_, T_pt_ = pairP[v]
                            for mm in (2 * v, 2 * v + 1):
                                c2, n2 = mm // nb, mm % nb
                                if n2 % gs == 0:
                                    pv, T_pv = ppv.next()
                                qcol = (n2 % gs) * 128
                                curP = (pt_, T_pt_, mm % 2)
                                if n2 > 0:
                                    S.add("pe", lambda e, pv=pv, qcol=qcol, mm=mm, l=l, pp_=prevP: e.matmul(
                                        pv[:, qcol:qcol + 128], vblk[l][:, mm - 1, :], pp_[0][:, pp_[2], 128:256],
                                        start=True, stop=False), [T_vblk[l], prevP[1]], [T_pv])
                                S.add("pe", lambda e, pv=pv, qcol=qcol, mm=mm, l=l, cp=curP, n2=n2: e.matmul(
                                    pv[:, qcol:qcol + 128], vblk[l][:, mm, :], cp[0][:, cp[2], 0:128],
                                    start=(n2 == 0), stop=True), [T_vblk[l], curP[1]], [T_pv])
                                prevP = curP
                                if n2 % gs == gs - 1:
                                    nbase = n2 - (gs - 1)
                                    st0 = 128 * nbase * r + c2
                                    dst = numacc[l][:, sl(st0, gs * 128, r)]
                                    if p == 0:
                                        S.add("act", lambda e, pv=pv, dst=dst, gs=gs: e.activation(out=dst, in_=pv[:, 0:gs * 128], func=AF.Copy),
                                              [T_pv], [T_numacc[l]])
                                    else:
                                        S.add("dve", lambda e, pv=pv, dst=dst, gs=gs: e.tensor_tensor(dst, pv[:, 0:gs * 128], dst, ALU.add),
                                              [T_pv, T_numacc[l]], [T_numacc[l]])
                if A_STOP == "vblk":
                    continue
                for cc in range(4):
                    ms, T_ms = mst.next()
                    for l in range(2):
                        nrow = slice(0, 64) if l == 0 else slice(64, 128)
                        drow = slice(64, 128) if l == 0 else slice(0, 64)
                        for c5 in range(2):
                            cs = slice(cc * 1024 + c5 * 512, cc * 1024 + (c5 + 1) * 512)
                            S.add("dve", lambda e, cs=cs, nrow=nrow, drow=drow, l=l: e.reciprocal(out=rdt[nrow, :], in_=numacc[l][drow, cs]),
                                  [T_numacc[l]], [T_rdt])
                            S.add("dve", lambda e, cs=cs, nrow=nrow, l=l, ms=ms, c5=c5: e.tensor_tensor(
                                ms[nrow, c5 * 512:(c5 + 1) * 512], numacc[l][nrow, cs], rdt[nrow, :], ALU.mult),
                                [T_numacc[l], T_rdt], [T_ms])
                    S.dma("pool", mixT_d[hp * 128:(hp + 1) * 128, cc * 1024:(cc + 1) * 1024], ms[:], [T_ms], [T_mixT_d[hp]])
            S.barrier()
        else:
            for j in range(4):
                S.dma("pool", mixT_d[j * 128:(j + 1) * 128, :], xnT[:, j, :], T_xnT, [T_mixT_d[j]])
        if "B" not in parts:
            for j in range(4, 8):
                S.dma("pool", mixT_d[j * 128:(j + 1) * 128, :], xnT[:, j, :], T_xnT, [T_mixT_d[j]])
            S.barrier()
            ph1.close()
        else:
            S.barrier()
            ph1.close()
            build_B()
        if "mixT" in dbg_d:
            for j in range(8):
                S.dma("pool", dbg_d["mixT"][j * 128:(j + 1) * 128, :], mixT_d[j * 128:(j + 1) * 128, :], [T_mixT_d[j]], [T_y])
            S.barrier()

        with contextlib.ExitStack() as p2:
            TCH = 256
            NCH = SEQ // TCH
            wo = sb(p2, "wo", [128, 8, D], BF16); T_wo = S.tt("wo")
            wg = sb(p2, "wg", [128, 8, FFN], BF16); T_wg = S.tt("wg")
            wu = sb(p2, "wu", [128, 8, FFN], BF16); T_wu = S.tt("wu")
            wd = sb(p2, "wd", [128, NHC, D], BF16); T_wd = S.tt("wd")
            gf = sb(p2, "gf", [128, D], F32); T_gf = S.tt("gf")
            S.dma("sp", gf[:], bass.AP(gf_d.tensor, 0, [[0, 128], [1, D]]), [T_in], [T_gf])
            f32b = Rot([(sb(p2, "f32b%d" % i, [128, D], F32), S.tt("f32b%d" % i)) for i in range(3)])
            cast_eng = Rot(["dve", "act"])

            def cast(eng, out, in_, scal, reads, writes):
                if eng == "act":
                    if scal is None:
                        S.add("act", lambda e: e.activation(out=out, in_=in_, func=AF.Copy), reads, writes)
                    else:
                        S.add("act", lambda e: e.activation(out=out, in_=in_, func=AF.Copy, scale=scal), reads, writes)
                else:
                    if scal is None:
                        S.add(eng, lambda e: e.tensor_copy(out, in_), reads, writes)
                    else:
                        S.add(eng, lambda e: e.tensor_scalar(out, in_, scal, None, ALU.mult), reads, writes)

            def load_w(dst, T_dst, src_d, nrow_chunks, ncols, gT, T_gT):
                for j in range(nrow_chunks):
                    for c0 in range(0, ncols, D):
                        cw = min(D, ncols - c0)
                        stg, T_stg = f32b.next()
                        S.dma("sp", stg[:, 0:cw], src_d[j * 128:(j + 1) * 128, c0:c0 + cw], [T_in], [T_stg])
                        eng = cast_eng.next()
                        if gT is None:
                            cast(eng, dst[:, j, c0:c0 + cw], stg[:, 0:cw], None, [T_stg], [T_dst])
                        else:
                            cast(eng, dst[:, j, c0:c0 + cw], stg[:, 0:cw], gT[:, j:j + 1], [T_stg, T_gT], [T_dst])

            load_w(wo, T_wo, w_out_d, 8, D, None, None)
            load_w(wg, T_wg, w_gate_d, 8, FFN, g2T, T_g2T)
            load_w(wu, T_wu, w_up_d, 8, FFN, g2T, T_g2T)
            load_w(wd, T_wd, w_down_d, NHC, D, None, None)

            mixc = Rot([(sb(p2, "mixc%d" % i, [128, 8, TCH], BF16), S.tt("mixc%d" % i)) for i in range(2)])
            h1 = sb(p2, "h1", [128, TCH // 128, D], F32); T_h1 = [S.tt("h1_%d" % i) for i in range(TCH // 128)]
            hnb = Rot([(sb(p2, "hnb%d" % i, [128, D], BF16), S.tt("hnb%d" % i)) for i in range(2)])
            hnT = sb(p2, "hnT", [128, 8, TCH], BF16); T_hnT = [S.tt("hnT%d" % i) for i in range(TCH // 128)]
            actT = sb(p2, "actT", [128, NHC, TCH], BF16); T_actT = [S.tt("actT%d" % i) for i in range(NHC)]
            sg = Rot([(sb(p2, "sg%d" % i, [128, TCH], F32), S.tt("sg%d" % i)) for i in range(2)])
            junk2 = sb(p2, "junk2", [128, D], BF16); T_junk2 = S.tt("junk2")
            st2 = sb(p2, "st2", [128, 8], F32); T_st2 = S.tt("st2")
            pyo = Rot([(ps(p2, "pyo%d" % i, [128, D], F32), S.tt("pyo%d" % i)) for i in range(2)])
            pgu = Rot([(ps(p2, "pgu%d" % i, [128, 2, TCH], F32), S.tt("pgu%d" % i)) for i in range(3)])
            ptr2 = ps(p2, "ptr2", [128, 8, 128], BF16); T_ptr2 = S.tt("ptr2")

            def rms_stats(src, T_src, col):
                S.add("act", lambda e: e.activation(out=junk2[:], in_=src, func=AF.Square, accum_out=st2[:, col:col + 1]),
                      [T_src], [T_junk2, T_st2])
                S.add("act", lambda e: e.activation(out=st2[:, col + 1:col + 2], in_=st2[:, col:col + 1], func=AF.Sqrt,
                                                    scale=1.0 / D, bias=EPS), [T_st2], [T_st2])
                S.add("dve", lambda e: e.reciprocal(out=st2[:, col + 2:col + 3], in_=st2[:, col + 1:col + 2]), [T_st2], [T_st2])
                return st2[:, col + 2:col + 3]

            for ch in range(NCH):
                t0 = ch * TCH
                mc, T_mc = mixc.next()
                S.dma("sp", mc[:], mixT_d.rearrange("(j p) t -> p j t", p=128)[:, :, t0:t0 + TCH], T_mixT_d, [T_mc])
                for tt in range(TCH // 128):
                    tok0 = t0 + tt * 128
                    xt, T_xt = f32b.next()
                    S.dma("sp", xt[:], x_d[tok0:tok0 + 128, :], [T_in], [T_xt])
                    yo, T_yo = pyo.next()
                    for half in range(2):
                        for j in range(8):
                            S.add("pe", lambda e, yo=yo, mc=mc, j=j, tt=tt, half=half: e.matmul(
                                yo[:, half * 512:(half + 1) * 512], mc[:, j, tt * 128:(tt + 1) * 128],
                                wo[:, j, half * 512:(half + 1) * 512], start=(j == 0), stop=(j == 7)),
                                [T_mc, T_wo], [T_yo])
                    S.add("dve", lambda e, yo=yo, xt=xt, tt=tt: e.tensor_tensor(h1[:, tt, :], yo[:], xt[:], ALU.add),
                          [T_yo, T_xt], [T_h1[tt]])
                    rstd = rms_stats(h1[:, tt, :], T_h1[tt], 0)
                    hb, T_hb = hnb.next()
                    S.add("dve", lambda e, hb=hb, tt=tt, rstd=rstd: e.tensor_scalar(hb[:], h1[:, tt, :], rstd, None, ALU.mult),
                          [T_h1[tt], T_st2], [T_hb])
                    for j in range(8):
                        S.add("pe", lambda e, hb=hb, j=j: e.transpose(ptr2[:, j, :], hb[:, j * 128:(j + 1) * 128], identb[:]),
                              [T_hb, T_identb], [T_ptr2])
                    S.add("act", lambda e, tt=tt: e.activation(out=hnT[:, :, tt * 128:(tt + 1) * 128], in_=ptr2[:], func=AF.Copy),
                          [T_ptr2], [T_hnT[tt]])
                for hc in range(NHC):
                    pg, T_pg = pgu.next()
                    for which, wsrc, T_w in ((0, wg, T_wg), (1, wu, T_wu)):
                        for j in range(8):
                            S.add("pe", lambda e, pg=pg, which=which, wsrc=wsrc, j=j, hc=hc: e.matmul(
                                pg[:, which, :], wsrc[:, j, hc * 128:(hc + 1) * 128], hnT[:, j, :],
                                start=(j == 0), stop=(j == 7)), [T_w] + T_hnT, [T_pg])
                    s_, T_s = sg.next()
                    S.add("act", lambda e, pg=pg, s_=s_: e.activation(out=s_[:], in_=pg[:, 0, :], func=AF.Silu), [T_pg], [T_s])
                    S.add("dve", lambda e, pg=pg, s_=s_, hc=hc: e.tensor_tensor(actT[:, hc, :], pg[:, 1, :], s_[:], ALU.mult),
                          [T_pg, T_s], [T_actT[hc]])
                for tt in range(TCH // 128):
                    tok0 = t0 + tt * 128
                    dn, T_dn = pyo.next()
                    for half in range(2):
                        for hc in range(NHC):
                            S.add("pe", lambda e, dn=dn, hc=hc, tt=tt, half=half: e.matmul(
                                dn[:, half * 512:(half + 1) * 512], actT[:, hc, tt * 128:(tt + 1) * 128],
                                wd[:, hc, half * 512:(half + 1) * 512], start=(hc == 0), stop=(hc == NHC - 1)),
                                T_actT + [T_wd], [T_dn])
                    S.add("dve", lambda e, dn=dn, tt=tt: e.tensor_tensor(h1[:, tt, :], dn[:], h1[:, tt, :], ALU.add),
                          [T_dn, T_h1[tt]], [T_h1[tt]])
                    rstd = rms_stats(h1[:, tt, :], T_h1[tt], 4)
                    ot, T_ot = f32b.next()
                    S.add("dve", lambda e, ot=ot, tt=tt, rstd=rstd: e.scalar_tensor_tensor(
                        ot[:], h1[:, tt, :], rstd, gf[:], ALU.mult, ALU.mult), [T_h1[tt], T_st2, T_gf], [T_ot])
                    S.dma("pool", y_d[tok0:tok0 + 128, :], ot[:], [T_ot], [T_y])
            S.barrier()
        S.emit(outer, verbose=True)
    return nc


_CONSTS = None


def make_in_maps(inputs, ncores=8):
    global _CONSTS
    if _CONSTS is None:
        _CONSTS = build_consts()
    f = lambda a: np.ascontiguousarray(np.asarray(a, dtype=np.float32))
    shared = {
        "norm1_g": f(inputs["norm1_g"]).reshape(1, D),
        "w_in": f(inputs["w_in"]).reshape(D, IN_W),
        "rel_bias": f(inputs["rel_bias"]),
        "cmp_pos": f(inputs["cmp_pos"]).reshape(32, HD),
        "cmp_k_w1": f(inputs["cmp_k_w1"]).reshape(32, HD, 128),
        "cmp_k_b1": f(inputs["cmp_k_b1"]).reshape(1, 128),
        "cmp_k_w2": f(inputs["cmp_k_w2"]).reshape(128, HD),
        "cmp_v_w1": f(inputs["cmp_v_w1"]).reshape(32, HD, 128),
        "cmp_v_b1": f(inputs["cmp_v_b1"]).reshape(1, 128),
        "cmp_v_w2": f(inputs["cmp_v_w2"]).reshape(128, HD),
        "w_out": f(inputs["w_out"]).reshape(D, D),
        "norm2_g": f(inputs["norm2_g"]).reshape(1, D),
        "w_gate": f(inputs["w_gate"]).reshape(D, FFN),
        "w_up": f(inputs["w_up"]).reshape(D, FFN),
        "w_down": f(inputs["w_down"]).reshape(FFN, D),
        "norm_f_g": f(inputs["norm_f_g"]).reshape(1, D),
    }
    shared.update(_CONSTS)
    x = f(inputs["x"])
    maps = []
    for b in range(ncores):
        m = dict(shared)
        m["x"] = np.ascontiguousarray(x[b])
        maps.append(m)
    return maps


def kernel(**inputs):
    nc = build_program()
    in_maps = make_in_maps(inputs, 8)
    res = run_bass_kernel_spmd(nc, in_maps, core_ids=list(range(8)))
    return np.stack([np.asarray(r["y"], dtype=np.float32) for r in res.results], axis=0)
```
